# Optimizing a Trainium2 kernel written in Bass

```python
import jax, jax.numpy as jnp
from jax import lax
import numpy as np

D_MODEL = 1024
BATCH = 8
SEQ = 8192
DEPTH = 2

GRID_W = 64
ROPE_THETA = 10000.0
EPS = 1e-6
Q_BLOCK = 128

MLA_HEADS = 8
MLA_Q_LORA = 256
MLA_KV_LORA = 128
MLA_NOPE = 64
MLA_ROPE = 32
MLA_V = 64
MLA_QK = MLA_NOPE + MLA_ROPE

GQA_HEADS = 8
GQA_KV_HEADS = 2
GQA_HEAD_DIM = 64

POOL_WINDOWS = (2, 4, 8, 16)
POOL_GROUPS = 4
POOL_GROUP = 128
POOL_WIDTH = POOL_GROUPS * POOL_GROUP

N_BRANCH = 3
MLA_OUT = MLA_HEADS * MLA_V
GQA_OUT = GQA_HEADS * GQA_HEAD_DIM

D_FF = 2816
CONV_W = 3

IN_SIZES = (MLA_Q_LORA, MLA_KV_LORA, MLA_ROPE,
            GQA_HEADS * GQA_HEAD_DIM, GQA_KV_HEADS * GQA_HEAD_DIM, GQA_KV_HEADS * GQA_HEAD_DIM,
            POOL_WIDTH, N_BRANCH * D_MODEL)
IN_WIDTH = (MLA_Q_LORA + MLA_KV_LORA + MLA_ROPE
            + (GQA_HEADS + 2 * GQA_KV_HEADS) * GQA_HEAD_DIM
            + POOL_WIDTH + N_BRANCH * D_MODEL)

kernel_name = 'hybrid_mla_gqa_pool_convffn_encoder'


def rms_norm(x, g):
    xf = x.astype(jnp.float32)
    y = xf * lax.rsqrt(jnp.mean(xf * xf, axis=-1, keepdims=True) + EPS)
    return (y * g.astype(jnp.float32)).astype(x.dtype)


def split_columns(y, sizes):
    out, start = [], 0
    for n in sizes:
        out.append(y[..., start:start + n])
        start += n
    return out


def axial_rope_tables(seq_len, rot_dim):
    rows = seq_len // GRID_W
    row_idx = jnp.repeat(jnp.arange(rows, dtype=jnp.float32), GRID_W)
    col_idx = jnp.tile(jnp.arange(GRID_W, dtype=jnp.float32), rows)
    n_axis = rot_dim // 4
    inv_freq = ROPE_THETA ** (-jnp.arange(n_axis, dtype=jnp.float32) / n_axis)
    ang = jnp.concatenate([row_idx[:, None] * inv_freq, col_idx[:, None] * inv_freq], axis=-1)
    return jnp.cos(ang), jnp.sin(ang)


def apply_rope(x, cos, sin):
    half = x.shape[-1] // 2
    x1, x2 = x[..., :half], x[..., half:]
    c = cos[None, :, None, :].astype(x.dtype)
    s = sin[None, :, None, :].astype(x.dtype)
    return jnp.concatenate([x1 * c - x2 * s, x1 * s + x2 * c], axis=-1)


def block_attention(q, k, v):
    b, s, h, dk = q.shape
    g = k.shape[2]
    rep = h // g
    dv = v.shape[-1]
    scale = dk ** -0.5
    nb = s // Q_BLOCK
    qb = q.reshape(b, nb, Q_BLOCK, g, rep, dk).transpose(1, 0, 2, 3, 4, 5)

    def one_block(qi):
        sc = jnp.einsum('bqgrd,bkgd->bgrqk', qi, k, preferred_element_type=jnp.float32) * scale
        p = jax.nn.softmax(sc, axis=-1).astype(v.dtype)
        return jnp.einsum('bgrqk,bkgv->bqgrv', p, v)

    out = lax.map(one_block, qb)
    return out.transpose(1, 0, 2, 3, 4, 5).reshape(b, s, h, dv)


def mla_branch(c_q, c_kv, k_rope_raw, q_norm_g, w_uq, kv_norm_g, w_ukv, qh_g, kh_g, cos, sin):
    b, s, _ = c_q.shape
    q = (rms_norm(c_q, q_norm_g) @ w_uq).reshape(b, s, MLA_HEADS, MLA_QK)
    kv = (rms_norm(c_kv, kv_norm_g) @ w_ukv).reshape(b, s, MLA_HEADS, MLA_NOPE + MLA_V)
    k_nope, v = kv[..., :MLA_NOPE], kv[..., MLA_NOPE:]
    k_rope = jnp.broadcast_to(k_rope_raw[:, :, None, :], (b, s, MLA_HEADS, MLA_ROPE))
    k = jnp.concatenate([k_nope, k_rope], axis=-1)
    q = rms_norm(q, qh_g)
    k = rms_norm(k, kh_g)
    q = jnp.concatenate([q[..., :MLA_NOPE], apply_rope(q[..., MLA_NOPE:], cos, sin)], axis=-1)
    k = jnp.concatenate([k[..., :MLA_NOPE], apply_rope(k[..., MLA_NOPE:], cos, sin)], axis=-1)
    o = block_attention(q, k, v)
    return o.reshape(b, s, MLA_OUT)


def gqa_branch(q_raw, k_raw, v_raw, qn_g, kn_g, cos, sin):
    b, s, _ = q_raw.shape
    q = q_raw.reshape(b, s, GQA_HEADS, GQA_HEAD_DIM)
    k = k_raw.reshape(b, s, GQA_KV_HEADS, GQA_HEAD_DIM)
    v = v_raw.reshape(b, s, GQA_KV_HEADS, GQA_HEAD_DIM)
    q = apply_rope(rms_norm(q, qn_g), cos, sin)
    k = apply_rope(rms_norm(k, kn_g), cos, sin)
    o = block_attention(q, k, v)
    return o.reshape(b, s, GQA_OUT)


def pool_branch(p, w_pool, pool_scale):
    b, s, _ = p.shape
    pg = p.reshape(b, s, POOL_GROUPS, POOL_GROUP)
    pf = pg.astype(jnp.float32)
    csum = jnp.concatenate([jnp.zeros((b, 1, POOL_GROUPS, POOL_GROUP), jnp.float32),
                            jnp.cumsum(pf, axis=1)], axis=1)
    t = jnp.arange(s)
    means = []
    for gi, w in enumerate(POOL_WINDOWS):
        lo = jnp.clip(t - w // 2, 0, s)
        hi = jnp.clip(t + w - w // 2, 0, s)
        cg = csum[:, :, gi]
        total = jnp.take(cg, hi, axis=1) - jnp.take(cg, lo, axis=1)
        means.append(total / (hi - lo).astype(jnp.float32)[None, :, None])
    pooled = jnp.stack(means, axis=2)
    mixed = (pooled - pf).astype(p.dtype)
    y = jnp.einsum('bsgc,gcd->bsgd', mixed, w_pool).reshape(b, s, POOL_WIDTH)
    return y * pool_scale


def depthwise_conv(u, w, bias):
    c = u.shape[-1]
    y = lax.conv_general_dilated(u, w[:, None, :], window_strides=(1,),
                                 padding=[(CONV_W // 2, CONV_W // 2)],
                                 dimension_numbers=('NWC', 'WIO', 'NWC'),
                                 feature_group_count=c)
    return y + bias


def setup_inputs(seed: int = 0) -> dict:
    key = jax.random.key(seed)
    ks = jax.random.split(key, 24)
    f32 = jnp.float32

    def dense(k, shape, fan_in):
        return jax.random.normal(k, shape, f32) * (fan_in ** -0.5)

    def gain(k, shape):
        return 1.0 + 0.1 * jax.random.normal(k, shape, f32)

    L = DEPTH
    conv_center = jnp.zeros((CONV_W, 1), f32).at[CONV_W // 2].set(1.0)
    return {
        'x': jax.random.normal(ks[0], (BATCH, SEQ, D_MODEL), f32),
        'attn_norm_g': gain(ks[1], (L, D_MODEL)),
        'w_in': dense(ks[2], (L, D_MODEL, IN_WIDTH), D_MODEL),
        'mla_q_norm_g': gain(ks[3], (L, MLA_Q_LORA)),
        'mla_w_uq': dense(ks[4], (L, MLA_Q_LORA, MLA_HEADS * MLA_QK), MLA_Q_LORA),
        'mla_kv_norm_g': gain(ks[5], (L, MLA_KV_LORA)),
        'mla_w_ukv': dense(ks[6], (L, MLA_KV_LORA, MLA_HEADS * (MLA_NOPE + MLA_V)), MLA_KV_LORA),
        'mla_q_head_g': gain(ks[7], (L, MLA_QK)),
        'mla_k_head_g': gain(ks[8], (L, MLA_QK)),
        'gqa_q_head_g': gain(ks[9], (L, GQA_HEAD_DIM)),
        'gqa_k_head_g': gain(ks[10], (L, GQA_HEAD_DIM)),
        'pool_w': dense(ks[11], (L, POOL_GROUPS, POOL_GROUP, POOL_GROUP), POOL_GROUP),
        'pool_scale': gain(ks[12], (L, POOL_WIDTH)),
        'w_branch_mla': dense(ks[13], (L, MLA_OUT, D_MODEL), MLA_OUT),
        'w_branch_gqa': dense(ks[14], (L, GQA_OUT, D_MODEL), GQA_OUT),
        'w_branch_pool': dense(ks[15], (L, POOL_WIDTH, D_MODEL), POOL_WIDTH),
        'w_out': dense(ks[16], (L, D_MODEL, D_MODEL), D_MODEL),
        'ffn_norm_g': gain(ks[17], (L, D_MODEL)),
        'ffn_w_up': dense(ks[18], (L, D_MODEL, 2 * D_FF), D_MODEL),
        'ffn_conv_w': conv_center[None] + 0.3 * jax.random.normal(ks[19], (L, CONV_W, 2 * D_FF), f32),
        'ffn_conv_b': 0.01 * jax.random.normal(ks[20], (L, 2 * D_FF), f32),
        'ffn_w_down': dense(ks[21], (L, D_FF, D_MODEL), D_FF),
    }


def reference(x, attn_norm_g, w_in, mla_q_norm_g, mla_w_uq, mla_kv_norm_g, mla_w_ukv,
              mla_q_head_g, mla_k_head_g, gqa_q_head_g, gqa_k_head_g, pool_w, pool_scale,
              w_branch_mla, w_branch_gqa, w_branch_pool, w_out, ffn_norm_g, ffn_w_up,
              ffn_conv_w, ffn_conv_b, ffn_w_down):
    b, s, d = x.shape
    cos_m, sin_m = axial_rope_tables(s, MLA_ROPE)
    cos_g, sin_g = axial_rope_tables(s, GQA_HEAD_DIM)
    for l in range(DEPTH):
        h = rms_norm(x, attn_norm_g[l])
        proj = h @ w_in[l]
        c_q, c_kv, k_rope, gq, gk, gv, pool_in, gate_logits = split_columns(proj, IN_SIZES)
        o_mla = mla_branch(c_q, c_kv, k_rope, mla_q_norm_g[l], mla_w_uq[l], mla_kv_norm_g[l],
                           mla_w_ukv[l], mla_q_head_g[l], mla_k_head_g[l], cos_m, sin_m)
        o_gqa = gqa_branch(gq, gk, gv, gqa_q_head_g[l], gqa_k_head_g[l], cos_g, sin_g)
        o_pool = pool_branch(pool_in, pool_w[l], pool_scale[l])
        gates = jax.nn.sigmoid(gate_logits.astype(jnp.float32)).astype(x.dtype).reshape(b, s, N_BRANCH, d)
        merged = (gates[:, :, 0] * (o_mla @ w_branch_mla[l])
                  + gates[:, :, 1] * (o_gqa @ w_branch_gqa[l])
                  + gates[:, :, 2] * (o_pool @ w_branch_pool[l]))
        x = x + merged @ w_out[l]
        hf = rms_norm(x, ffn_norm_g[l])
        u = depthwise_conv(hf @ ffn_w_up[l], ffn_conv_w[l], ffn_conv_b[l])
        u_gate, u_val = u[..., :D_FF], u[..., D_FF:]
        x = x + (jax.nn.silu(u_gate) * u_val) @ ffn_w_down[l]
    return x
```

```python
import contextlib
import itertools
_UID = itertools.count()
import os
CUT = int(os.environ.get('K_CUT', '99'))
import numpy as np
import concourse.bass as bass
import concourse.mybir as mybir
from concourse.bass_utils import run_bass_kernel_spmd

F32 = mybir.dt.float32
BF16 = mybir.dt.bfloat16
AF = mybir.ActivationFunctionType
ALU = mybir.AluOpType
AX = mybir.AxisListType

D = 1024
DEPTH = 2
SEQ = 8192
NCORES = 8
EPS = 1e-6
IN_W = 4768
DFF = 2816
GRID_W = 64
ROPE_THETA = 10000.0
POOL_WINDOWS = (2, 4, 8, 16)
WARM_N = 0


class Res:
    __slots__ = ("w", "r", "name", "psum")

    def __init__(self, name="", psum=False):
        self.w = None
        self.r = {}
        self.name = name
        self.psum = psum


def PRes():
    return Res(psum=True)


class TR:
    def __init__(self, nc, es):
        self.nc = nc
        self.es = es
        self.eng = {"pe": nc.tensor, "act": nc.scalar, "dve": nc.vector, "pool": nc.gpsimd, "sp": (nc.gpsimd if os.environ.get("K_SWDGE") else nc.sync)}
        self.sems = {}
        self.cnt = {}
        self.waited = {e: {} for e in self.eng}
        self.gen = 0
        self.pending = None
        self.ekey = {}
        self._new_engine_sems()

    def _new_engine_sems(self):
        self.gen += 1
        for e in self.eng:
            key = "e%d_%s" % (self.gen, e)
            self.sems[key] = self.es.enter_context(self.nc.semaphore(key))
            self.cnt[key] = 0
            self.ekey[e] = key

    def dsem(self, name):
        key = "d_" + name
        if key not in self.sems:
            self.sems[key] = self.es.enter_context(self.nc.semaphore(key))
            self.cnt[key] = 0
        return key

    def _wait(self, e, key, val):
        if val <= 0 or self.waited[e].get(key, 0) >= val:
            return
        if key == self.ekey[e] and (e == "pe" or os.environ.get("K_NOOWN")):
            return
        if self.pending is not None:
            self.pending = [(k, v) for (k, v) in self.pending if k != key] + [(key, val)]
            self.waited[e][key] = val
            return
        self.eng[e].wait_ge(self.sems[key], val)
        self.waited[e][key] = val

    def _deps(self, e, reads, writes, skip_own=False):
        if skip_own:
            own = self.ekey.get(e)
            sv = self.waited[e].get(own, 0)
            self.waited[e][own] = 1 << 60
            try:
                self._deps(e, reads, writes)
            finally:
                self.waited[e][own] = sv
            return
        for r in reads:
            if r.w:
                self._wait(e, *r.w)
            if r.psum:
                own = self.ekey.get(e)
                for k, v in r.r.items():
                    if k != own:
                        self._wait(e, k, v)
        for r in writes:
            if r.w:
                self._wait(e, *r.w)
            for k, v in r.r.items():
                self._wait(e, k, v)

    def _mark(self, key, v, reads, writes):
        for r in reads:
            if r.r.get(key, 0) < v:
                r.r[key] = v
        for r in writes:
            r.w = (key, v)
            r.r = {}

    def op(self, e, fn, reads=(), writes=(), skip_own=False, embed=True):
        if embed:
            self.pending = []
        self._deps(e, reads, writes, skip_own)
        emb = None
        if embed:
            pend, self.pending = self.pending, None
            for k, v in pend[:-1]:
                self.eng[e].wait_ge(self.sems[k], v)
            if pend:
                emb = pend[-1]
        ins = fn(self.eng[e])
        if emb is not None:
            ins._wait_ge(self.sems[emb[0]], emb[1])
        key = self.ekey[e]
        self.cnt[key] += 1
        ins.then_inc(self.sems[key], 1)
        self._mark(key, self.cnt[key], reads, writes)
        return ins

    def dma(self, q, sem, out, in_, reads=(), writes=(), **kw):
        self._deps(q, reads, writes)
        key = self.dsem(sem)
        ins = self.eng[q].dma_start(out=out, in_=in_, **kw)
        self.cnt[key] += 16
        ins.then_inc(self.sems[key], 16)
        self._mark(key, self.cnt[key], reads, writes)
        return ins

    def barrier(self):
        for e in self.eng:
            for key, c in self.cnt.items():
                self._wait(e, key, c)
        if max(self.cnt[k] for k in self.ekey.values()) > 100000:
            self._new_engine_sems()


def host_consts(S):
    rows = S // GRID_W
    row_idx = np.repeat(np.arange(rows, dtype=np.float32), GRID_W)
    col_idx = np.tile(np.arange(GRID_W, dtype=np.float32), rows)

    def tables(rot_dim):
        n_axis = rot_dim // 4
        inv_freq = (np.float32(ROPE_THETA) ** (-np.arange(n_axis, dtype=np.float32) / np.float32(n_axis))).astype(np.float32)
        ang = np.concatenate([row_idx[:, None] * inv_freq, col_idx[:, None] * inv_freq], axis=-1).astype(np.float32)
        return np.cos(ang).astype(np.float32), np.sin(ang).astype(np.float32)

    cm, sm = tables(32)
    cg, sg = tables(64)
    t = np.arange(S)
    invc = np.zeros((4, S), np.float32)
    for gi, w in enumerate(POOL_WINDOWS):
        lo = np.clip(t - w // 2, 0, S)
        hi = np.clip(t + w - w // 2, 0, S)
        invc[gi] = (1.0 / (hi - lo).astype(np.float32)).astype(np.float32)
    return {
        "c_ident": np.eye(128, dtype=np.float32),
        "c_cm": np.ascontiguousarray(np.tile(cm, (1, 8))),
        "c_sm": np.ascontiguousarray(np.tile(sm, (1, 8))),
        "c_cg": np.ascontiguousarray(np.tile(cg, (1, 8))),
        "c_sg": np.ascontiguousarray(np.tile(sg, (1, 8))),
        "c_invc": np.ascontiguousarray(np.broadcast_to(invc[None], (128, 4, S))) if False else invc,
    }


def host_layout(inp, l_count):
    o = {}
    f = lambda a: np.ascontiguousarray(a, dtype=np.float32)
    L = l_count
    o["p_attn_g"] = f(inp["attn_norm_g"].reshape(L, 8, 128).transpose(0, 2, 1))
    o["p_ffn_g"] = f(inp["ffn_norm_g"].reshape(L, 8, 128).transpose(0, 2, 1))
    o["p_gq"] = f(inp["mla_q_norm_g"].reshape(L, 2, 128).transpose(0, 2, 1))
    o["p_gkv"] = f(inp["mla_kv_norm_g"].reshape(L, 1, 128).transpose(0, 2, 1))
    o["p_pscale"] = f(inp["pool_scale"].reshape(L, 4, 128).transpose(0, 2, 1))
    o["p_convw"] = f(inp["ffn_conv_w"].reshape(L, 3, 44, 128).transpose(0, 3, 1, 2))
    o["p_convb"] = f(inp["ffn_conv_b"].reshape(L, 44, 128).transpose(0, 2, 1))
    rep = lambda v, n: np.broadcast_to(np.tile(v, (1, n))[:, None, :], (L, 128, v.shape[1] * n))
    o["p_Gq"] = f(rep(inp["mla_q_head_g"], 8))
    o["p_Gkn"] = f(rep(inp["mla_k_head_g"][:, :64], 8))
    o["p_Gkr"] = f(rep(inp["mla_k_head_g"][:, 64:], 8))
    o["p_Ggq"] = f(rep(inp["gqa_q_head_g"], 8))
    o["p_Ggk"] = f(rep(inp["gqa_k_head_g"], 2))
    return o


def build_nc(S=SEQ, L=DEPTH, debug_stop=None):
    assert S % 1024 == 0
    nc = bass.Bass("TRN2", target_bir_lowering=False)
    NT = S // 512

    def din(name, shape, dt=F32):
        return nc.dram_tensor(name, list(shape), dt, kind="ExternalInput").ap()

    def dout(name, shape, dt=F32):
        return nc.dram_tensor(name, list(shape), dt, kind="ExternalOutput").ap()

    x_in = din("x", [S, D])
    w_in = din("w_in", [L, D, IN_W])
    w_uq = din("mla_w_uq", [L, 256, 768])
    w_ukv = din("mla_w_ukv", [L, 128, 1024])
    pool_w = din("pool_w", [L, 4, 128, 128])
    w_bm = din("w_branch_mla", [L, 512, D])
    w_bg = din("w_branch_gqa", [L, 512, D])
    w_bp = din("w_branch_pool", [L, 512, D])
    w_out = din("w_out", [L, D, D])
    w_up = din("ffn_w_up", [L, D, 2 * DFF])
    w_down = din("ffn_w_down", [L, DFF, D])
    p_attn_g = din("p_attn_g", [L, 128, 8])
    p_ffn_g = din("p_ffn_g", [L, 128, 8])
    p_gq = din("p_gq", [L, 128, 2])
    p_gkv = din("p_gkv", [L, 128, 1])
    p_pscale = din("p_pscale", [L, 128, 4])
    p_convw = din("p_convw", [L, 128, 3, 44])
    p_convb = din("p_convb", [L, 128, 44])
    p_Gq = din("p_Gq", [L, 128, 768])
    p_Gkn = din("p_Gkn", [L, 128, 512])
    p_Gkr = din("p_Gkr", [L, 128, 256])
    p_Ggq = din("p_Ggq", [L, 128, 512])
    p_Ggk = din("p_Ggk", [L, 128, 128])
    c_ident = din("c_ident", [128, 128])
    c_cm = din("c_cm", [S, 128])
    c_sm = din("c_sm", [S, 128])
    c_cg = din("c_cg", [S, 256])
    c_sg = din("c_sg", [S, 256])
    c_invc = din("c_invc", [4, S])

    out = dout("out", [S, D])
    hT_d = dout("s_hT", [D, S], BF16)
    qTm_d = dout("s_qTm", [8, 96, S], BF16)
    kTm_d = dout("s_kTm", [8, 96, S], BF16)
    vm_d = dout("s_vm", [S, 512], BF16)
    qTg_d = dout("s_qTg", [512, S], BF16)
    kTg_d = dout("s_kTg", [128, S], BF16)
    vg_d = dout("s_vg", [S, 128], BF16)
    pT_d = dout("s_pT", [512, S], F32)
    oT_d = dout("s_oT", [1024, S], BF16)
    rec_d = dout("s_rec", [2, 1024], F32)

    es = contextlib.ExitStack()
    with es:
        tr = TR(nc, es)
        sbg = lambda n, s, d: es.enter_context(nc.sbuf_tensor(n, list(s), d))
        ident = sbg("ident", [128, 128], BF16)
        ones_f = sbg("ones_f", [128, 128], F32)
        zeros = sbg("zeros", [128, 16], F32)
        R_const = Res("const")

        with contextlib.ExitStack() as ps:
            identf = ps.enter_context(nc.sbuf_tensor("identf", [128, 128], F32))
            r_idf = Res()
            tr.dma("sp", "c0", identf[:], c_ident, writes=[r_idf])
            tr.op("dve", lambda e: e.tensor_copy(out=ident[:], in_=identf[:]), reads=[r_idf], writes=[R_const])
            tr.op("dve", lambda e: e.memset(ones_f[:], 1.0), writes=[R_const])
            tr.op("dve", lambda e: e.memset(zeros[:], 0.0), writes=[R_const])
            tr.barrier()

        for l in range(L):
            x_src = x_in if l == 0 else out
            phase1(nc, tr, l, S, x_src, w_in, w_uq, w_ukv, p_attn_g, p_gq, p_gkv, p_Gq, p_Gkn, p_Gkr, p_Ggq, p_Ggk,
                   c_cm, c_sm, c_cg, c_sg, ident, ones_f, hT_d, qTm_d, kTm_d, vm_d, qTg_d, kTg_d, vg_d, pT_d)
            tr.barrier()
            if debug_stop == (l, 1):
                break
            phase2(nc, tr, S, ident, ones_f, qTm_d, kTm_d, vm_d, qTg_d, kTg_d, vg_d, oT_d, rec_d)
            tr.barrier()
            if debug_stop == (l, 2):
                break
            phase3(nc, tr, l, S, x_src, out, w_in, pool_w, w_bm, w_bg, w_bp, w_out, p_pscale, p_ffn_g, c_invc,
                   ident, hT_d, pT_d, oT_d)
            tr.barrier()
            if debug_stop == (l, 3):
                break
            phase4(nc, tr, l, S, out, w_up, w_down, p_convw, p_convb, hT_d)
            tr.barrier()
    return nc


def load_cast(nc, tr, ps, name, dst, dst_res, src_rows_ap, ncols, nchunks_k, stg, stg_res, state):
    CH = stg[0].shape[1]
    for k in range(nchunks_k):
        c0 = 0
        if os.environ.get('K_NOWAW'):
            dst_res = Res()
        while c0 < ncols:
            w = min(CH, ncols - c0)
            i = state[0] % len(stg)
            state[0] += 1
            tr.dma("sp", "stg%d" % i, stg[i][:, 0:w], src_rows_ap[k * 128:(k + 1) * 128, c0:c0 + w], writes=[stg_res[i]])
            if True:
                tr.op("dve", lambda e, i=i, k=k, c0=c0, w=w: e.tensor_copy(out=dst[:, k, c0:c0 + w], in_=stg[i][:, 0:w]),
                      reads=[stg_res[i]], writes=[dst_res])
            else:
                tr.op("act", lambda e, i=i, k=k, c0=c0, w=w: e.activation(out=dst[:, k, c0:c0 + w], in_=stg[i][:, 0:w], func=AF.Copy),
                      reads=[stg_res[i]], writes=[dst_res])
            c0 += w


def phase1(nc, tr, l, S, x_src, w_in, w_uq, w_ukv, p_attn_g, p_gq, p_gkv, p_Gq, p_Gkn, p_Gkr, p_Ggq, p_Ggk,
           c_cm, c_sm, c_cg, c_sg, ident, ones_f, hT_d, qTm_d, kTm_d, vm_d, qTg_d, kTg_d, vg_d, pT_d):
    NT = S // 512
    with contextlib.ExitStack() as ps:
        uid = "p1_%d_" % next(_UID)
        sb = lambda n, s, d: ps.enter_context(nc.sbuf_tensor(uid + n, list(s), d))
        pm = lambda n, s, d: ps.enter_context(nc.psum_tensor(uid + n, list(s), d))
        if CUT < 0:
            return
        wi = sb("wi", [128, 8, 1696], BF16); r_wi = Res()
        wq = sb("wq", [128, 2, 768], BF16); r_wq = Res()
        wkv = sb("wkv", [128, 1, 1024], BF16); r_wkv = Res()
        with contextlib.ExitStack() as ps2:
            stg = [ps2.enter_context(nc.sbuf_tensor(uid + "stg%d" % i, [128, 1696], F32)) for i in range(2)]
            stg_res = [Res(), Res()]
            st = [0]
            load_cast(nc, tr, ps, "wi", wi, r_wi, w_in[l][:, 0:1696], 1696, int(os.environ.get("K_NCH", "8")), stg, stg_res, st)
            if os.environ.get("K_SKIP") != "wq":
                load_cast(nc, tr, ps, "wq", wq, r_wq, w_uq[l], 768, 2, stg, stg_res, st)
            if os.environ.get("K_SKIP") != "wkv":
                load_cast(nc, tr, ps, "wkv", wkv, r_wkv, w_ukv[l], 1024, 1, stg, stg_res, st)
            tr.barrier()
        if CUT < 1:
            return
        g_attn = sb("g_attn", [128, 8], F32); g_q = sb("g_q", [128, 2], F32); g_kv = sb("g_kv", [128, 1], F32)
        Gq = sb("Gq", [128, 8, 96], F32); Gkn = sb("Gkn", [128, 8, 64], F32); Gkr = sb("Gkr", [128, 8, 32], F32)
        Ggq = sb("Ggq", [128, 8, 64], F32); Ggk = sb("Ggk", [128, 2, 64], F32)
        r_par = Res()
        tr.dma("sp", "par", g_attn[:], p_attn_g[l], writes=[r_par])
        tr.dma("sp", "par", g_q[:], p_gq[l], writes=[r_par])
        tr.dma("sp", "par", g_kv[:], p_gkv[l], writes=[r_par])
        tr.dma("sp", "par", Gq[:].rearrange("p h d -> p (h d)"), p_Gq[l], writes=[r_par])
        tr.dma("sp", "par", Gkn[:].rearrange("p h d -> p (h d)"), p_Gkn[l], writes=[r_par])
        tr.dma("sp", "par", Gkr[:].rearrange("p h d -> p (h d)"), p_Gkr[l], writes=[r_par])
        tr.dma("sp", "par", Ggq[:].rearrange("p h d -> p (h d)"), p_Ggq[l], writes=[r_par])
        tr.dma("sp", "par", Ggk[:].rearrange("p h d -> p (h d)"), p_Ggk[l], writes=[r_par])

        xt = [sb("xt%d" % i, [128, 4, 1024], F32) for i in range(2)]; r_xt = [Res(), Res()]
        junk = sb("junk", [128, 1024], F32); r_junk = Res()
        ss = sb("ss", [128, 4], F32); r_ss = Res()
        rstd = sb("rstd", [128, 4], F32); r_rstd = Res()
        xn = sb("xn", [128, 4, 1024], BF16); r_xn = Res()
        hT = [sb("hT%d" % i, [128, 8, 512], BF16) for i in range(2)]; r_hT = [Res(), Res()]
        cqg = sb("cqg", [128, 2, 512], BF16); r_cqg = Res()
        ckvg = sb("ckvg", [128, 512], BF16); r_ckvg = Res()
        cqsq = sb("cqsq", [128, 2, 512], F32); r_cqsq = Res()
        ckvsq = sb("ckvsq", [128, 512], F32); r_ckvsq = Res()
        pst = [sb("pst%d" % i, [128, 512], F32) for i in range(2)]; r_pst = [Res(), Res()]
        cm = sb("cm", [128, 4, 128], F32); sm = sb("sm", [128, 4, 128], F32)
        cg = sb("cg", [128, 4, 256], F32); sg = sb("sg", [128, 4, 256], F32); r_tab = Res()
        rr = sb("rr", [128, 2], F32); r_rr = Res()
        hs = sb("hs", [128, 8], F32); r_hs = Res()
        sc = sb("sc", [128, 8], F32); r_sc = Res()
        rs1 = sb("rs1", [128, 1], F32); r_rs1 = Res()
        qn = sb("qn", [128, 8, 96], F32); r_qn = Res()
        t1 = sb("t1", [128, 8, 32], F32); t2 = sb("t2", [128, 8, 32], F32); r_t1 = Res(); r_t2 = Res()
        qb = sb("qb", [128, 8, 96], BF16); r_qb = Res()
        kb = sb("kb", [128, 8, 96], BF16); r_kb = Res()
        kr = sb("kr", [128, 8, 32], F32); r_kr = Res()
        gqn = sb("gqn", [128, 8, 64], F32); r_gqn = Res()
        gqb = sb("gqb", [128, 8, 64], BF16); r_gqb = Res()
        gkn = sb("gkn", [128, 2, 64], F32); r_gkn = Res()
        gkb = sb("gkb", [128, 2, 64], BF16); r_gkb = Res()
        qT_st = sb("qT_st", [128, 8, 512], BF16); r_qT_st = Res()
        kT_st = sb("kT_st", [128, 8, 512], BF16); r_kT_st = Res()
        gqT_st = sb("gqT_st", [128, 4, 512], BF16); r_gqT_st = Res()
        gkT_st = sb("gkT_st", [128, 512], BF16); r_gkT_st = Res()
        v_st = sb("v_st", [128, 4, 512], BF16); r_v_st = Res()
        gv_st = sb("gv_st", [128, 4, 128], BF16); r_gv_st = Res()
        tp = [pm("tp%d" % i, [128, 1024], BF16) for i in range(2)]; r_tp = [PRes(), PRes()]
        fm = [pm("fm%d" % i, [128, 512], F32) for i in range(2)]; r_fm = [PRes(), PRes()]
        A = pm("A", [128, 1024], F32); r_A = PRes(); r_ssq = r_A
        B = pm("B", [128, 1024], F32); r_B = PRes()
        tpi = [0]
        fmi = [0]

        def load_x(T):
            b = T % 2
            tr.dma("sp", "x%d" % b, xt[b][:], x_src[T * 512:(T + 1) * 512, :].rearrange("(s p) d -> p s d", p=128),
                   writes=[r_xt[b]])

        if CUT < 2:
            return
        load_x(0)
        for T in range(NT):
            b = T % 2
            if T + 1 < NT:
                load_x(T + 1)
            tsl = slice(T * 512, (T + 1) * 512)
            tr.dma("sp", "tab", cm[:], c_cm[tsl, :].rearrange("(s p) d -> p s d", p=128), writes=[r_tab])
            tr.dma("sp", "tab", sm[:], c_sm[tsl, :].rearrange("(s p) d -> p s d", p=128), writes=[r_tab])
            tr.dma("sp", "tab", cg[:], c_cg[tsl, :].rearrange("(s p) d -> p s d", p=128), writes=[r_tab])
            tr.dma("sp", "tab", sg[:], c_sg[tsl, :].rearrange("(s p) d -> p s d", p=128), writes=[r_tab])
            for s in range(4):
                tr.op("act", lambda e, s=s: e.activation(out=junk[:], in_=xt[b][:, s, :], func=AF.Square,
                                                          accum_out=ss[:, s:s + 1]),
                      reads=[r_xt[b]], writes=[r_junk, r_ss])
            tr.op("act", lambda e: e.activation(out=rstd[:], in_=ss[:], func=AF.Sqrt, scale=1.0 / D, bias=EPS),
                  reads=[r_ss], writes=[r_rstd])
            tr.op("dve", lambda e: e.reciprocal(out=rstd[:], in_=rstd[:]), reads=[r_rstd], writes=[r_rstd])
            for s in range(4):
                tr.op("dve", lambda e, s=s: e.tensor_scalar(out=xn[:, s, :], in0=xt[b][:, s, :], scalar1=rstd[:, s:s + 1],
                                                            scalar2=None, op0=ALU.mult),
                      reads=[r_xt[b], r_rstd], writes=[r_xn])
            for r in range(4):
                ti = tpi[0] % 2; tpi[0] += 1
                for cc in range(2):
                    for s in range(4):
                        c = 2 * r + cc
                        tr.op("pe", lambda e, c=c, s=s, cc=cc, ti=ti: e.transpose(
                            out=tp[ti][:, cc * 512 + s * 128: cc * 512 + (s + 1) * 128],
                            in_=xn[:, s, c * 128:(c + 1) * 128], identity=ident[:]),
                            reads=[r_xn, R_const_dummy], writes=[r_tp[ti]])
                for cc in range(2):
                    c = 2 * r + cc
                    eng = "act" if cc == 0 else "dve"
                    if eng == "act":
                        tr.op("act", lambda e, c=c, cc=cc, ti=ti: e.activation(out=hT[b][:, c, :], in_=tp[ti][:, cc * 512:(cc + 1) * 512],
                                                                                func=AF.Copy, scale=g_attn[:, c:c + 1]),
                              reads=[r_tp[ti], r_par], writes=[r_hT[b]])
                    else:
                        tr.op("dve", lambda e, c=c, cc=cc, ti=ti: e.tensor_scalar(out=hT[b][:, c, :], in0=tp[ti][:, cc * 512:(cc + 1) * 512],
                                                                                   scalar1=g_attn[:, c:c + 1], scalar2=None, op0=ALU.mult),
                              reads=[r_tp[ti], r_par], writes=[r_hT[b]])
            tr.dma("sp", "hTst%d" % b, hT_d[:, T * 512:(T + 1) * 512].rearrange("(c p) t -> p c t", p=128), hT[b][:],
                   reads=[r_hT[b]])
            if CUT < 3:
                continue
            fm_groups = [("cq", 0, 0), ("cq", 1, 128), ("ckv", 0, 256)] + [("pool", g, 1184 + g * 128) for g in range(4)]
            SUB = os.environ.get('K_SUB', '')
            for kind, idx, col in fm_groups:
                if kind in SUB.split(','):
                    continue
                fi = fmi[0] % 2; fmi[0] += 1
                for c in range(8):
                    tr.op("pe", lambda e, c=c, col=col, fi=fi: e.matmul(fm[fi][:], lhsT=wi[:, c, col:col + 128], rhs=hT[b][:, c, :],
                                                                       start=(c == 0), stop=(c == 7)),
                          reads=[r_wi, r_hT[b]], writes=[r_fm[fi]])
                if kind == "cq":
                    tr.op("act", lambda e, idx=idx, fi=fi: e.activation(out=cqsq[:, idx, :], in_=fm[fi][:], func=AF.Square),
                          reads=[r_fm[fi]], writes=[r_cqsq])
                    tr.op("dve", lambda e, idx=idx, fi=fi: e.tensor_scalar(out=cqg[:, idx, :], in0=fm[fi][:], scalar1=g_q[:, idx:idx + 1],
                                                                           scalar2=None, op0=ALU.mult),
                          reads=[r_fm[fi], r_par], writes=[r_cqg])
                elif kind == "ckv":
                    tr.op("act", lambda e, fi=fi: e.activation(out=ckvsq[:], in_=fm[fi][:], func=AF.Square),
                          reads=[r_fm[fi]], writes=[r_ckvsq])
                    tr.op("dve", lambda e, fi=fi: e.tensor_scalar(out=ckvg[:], in0=fm[fi][:], scalar1=g_kv[:, 0:1], scalar2=None, op0=ALU.mult),
                          reads=[r_fm[fi], r_par], writes=[r_ckvg])
                else:
                    pi = idx % 2
                    tr.op("act", lambda e, fi=fi, pi=pi: e.activation(out=pst[pi][:], in_=fm[fi][:], func=AF.Copy),
                          reads=[r_fm[fi]], writes=[r_pst[pi]])
                    tr.dma("sp", "pst%d" % pi, pT_d[idx * 128:(idx + 1) * 128, T * 512:(T + 1) * 512], pst[pi][:],
                           reads=[r_pst[pi]])
            if CUT < 4:
                continue
            for s in range(4):
                ssl = slice(s * 128, (s + 1) * 128)
                for (o0, c0, w) in ((0, 384, 512), (512, 896, 288)):
                    for c in range(8):
                        tr.op("pe", lambda e, c=c, o0=o0, c0=c0, w=w: e.matmul(A[:, o0:o0 + w], lhsT=hT[b][:, c, ssl], rhs=wi[:, c, c0:c0 + w],
                                                                              start=(c == 0), stop=(c == 7)),
                              reads=[r_wi, r_hT[b]], writes=[r_A])
                for m in range(2):
                    tr.op("pe", lambda e, m=m: e.matmul(A[:, 800:801], lhsT=cqsq[:, m, ssl], rhs=ones_f[:, 0:1], start=(m == 0), stop=(m == 1)),
                          reads=[r_cqsq], writes=[r_ssq])
                tr.op("pe", lambda e: e.matmul(A[:, 801:802], lhsT=ckvsq[:, ssl], rhs=ones_f[:, 0:1], start=True, stop=True),
                      reads=[r_ckvsq], writes=[r_ssq])
                for (o0, w) in ((0, 512), (512, 256)):
                    for m in range(2):
                        tr.op("pe", lambda e, m=m, o0=o0, w=w: e.matmul(B[:, o0:o0 + w], lhsT=cqg[:, m, ssl], rhs=wq[:, m, o0:o0 + w],
                                                                       start=(m == 0), stop=(m == 1)),
                              reads=[r_wq, r_cqg], writes=[r_B])
                tr.op("act", lambda e: e.activation(out=rr[:, 0:1], in_=A[:, 800:801], func=AF.Sqrt, scale=1.0 / 256, bias=EPS),
                      reads=[r_ssq], writes=[r_rr])
                tr.op("act", lambda e: e.activation(out=rr[:, 1:2], in_=A[:, 801:802], func=AF.Sqrt, scale=1.0 / 128, bias=EPS),
                      reads=[r_ssq], writes=[r_rr])
                tr.op("dve", lambda e: e.reciprocal(out=rr[:], in_=rr[:]), reads=[r_rr], writes=[r_rr])
                if CUT < 5:
                    continue
                Bq = B[:, 0:768].rearrange("p (h d) -> p h d", h=8)
                tr.op("act", lambda e: e.activation(out=qn[:], in_=Bq, func=AF.Square, scale=rr[:, 0:1]),
                      reads=[r_B, r_rr], writes=[r_qn])
                tr.op("dve", lambda e: e.reduce_sum(out=hs[:], in_=qn[:], axis=AX.X), reads=[r_qn], writes=[r_hs])
                tr.op("act", lambda e: e.activation(out=hs[:], in_=hs[:], func=AF.Sqrt, scale=1.0 / 96, bias=EPS), reads=[r_hs], writes=[r_hs])
                tr.op("dve", lambda e: e.reciprocal(out=hs[:], in_=hs[:]), reads=[r_hs], writes=[r_hs])
                tr.op("dve", lambda e: e.tensor_scalar(out=sc[:], in0=hs[:], scalar1=rr[:, 0:1], scalar2=None, op0=ALU.mult),
                      reads=[r_hs, r_rr], writes=[r_sc])
                for h in range(8):
                    tr.op("dve", lambda e, h=h: e.scalar_tensor_tensor(out=qn[:, h, :], in0=B[:, h * 96:(h + 1) * 96], scalar=sc[:, h:h + 1],
                                                                       in1=Gq[:, h, :], op0=ALU.mult, op1=ALU.mult),
                          reads=[r_B, r_sc, r_par], writes=[r_qn])
                for (o0, w) in ((0, 512), (512, 512)):
                    tr.op("pe", lambda e, o0=o0, w=w: e.matmul(B[:, o0:o0 + w], lhsT=ckvg[:, ssl], rhs=wkv[:, 0, o0:o0 + w], start=True, stop=True),
                          reads=[r_wkv, r_ckvg], writes=[r_B])
                cmv = cm[:, s, :].rearrange("p (h d) -> p h d", h=8)
                smv = sm[:, s, :].rearrange("p (h d) -> p h d", h=8)
                rope(tr, "dve", qn, r_qn, 64, 16, cmv, smv, r_tab, t1, r_t1, t2, r_t2, qb, r_qb)
                tr.op("dve", lambda e: e.tensor_copy(out=qb[:, :, 0:64], in_=qn[:, :, 0:64]), reads=[r_qn], writes=[r_qb])
                ti = tpi[0] % 2; tpi[0] += 1
                for h in range(8):
                    tr.op("pe", lambda e, h=h, ti=ti: e.transpose(out=tp[ti][0:96, h * 128:(h + 1) * 128], in_=qb[:, h, :], identity=ident[:]),
                          reads=[r_qb], writes=[r_tp[ti]])
                tr.op("act", lambda e, ti=ti: e.activation(out=qT_st[0:96, :, ssl], in_=tp[ti][0:96, :].rearrange("p (h t) -> p h t", h=8), func=AF.Copy),
                      reads=[r_tp[ti]], writes=[r_qT_st])
                if CUT < 6:
                    continue
                Bkv = B[:].rearrange("p (h d) -> p h d", h=8)
                tr.op("act", lambda e: e.activation(out=junk[:, 0:32], in_=A[:, 0:32], func=AF.Square, accum_out=rs1[:, 0:1]),
                      reads=[r_A], writes=[r_junk, r_rs1])
                tr.op("act", lambda e: e.activation(out=gqn[:], in_=Bkv[:, :, 0:64], func=AF.Square, scale=rr[:, 1:2]),
                      reads=[r_B, r_rr], writes=[r_gqn])
                tr.op("dve", lambda e: e.reduce_sum(out=hs[:], in_=gqn[:], axis=AX.X), reads=[r_gqn], writes=[r_hs])
                tr.op("dve", lambda e: e.tensor_scalar(out=hs[:], in0=hs[:], scalar1=rs1[:, 0:1], scalar2=None, op0=ALU.add),
                      reads=[r_hs, r_rs1], writes=[r_hs])
                tr.op("act", lambda e: e.activation(out=hs[:], in_=hs[:], func=AF.Sqrt, scale=1.0 / 96, bias=EPS), reads=[r_hs], writes=[r_hs])
                tr.op("dve", lambda e: e.reciprocal(out=hs[:], in_=hs[:]), reads=[r_hs], writes=[r_hs])
                tr.op("dve", lambda e: e.tensor_scalar(out=sc[:], in0=hs[:], scalar1=rr[:, 1:2], scalar2=None, op0=ALU.mult),
                      reads=[r_hs, r_rr], writes=[r_sc])
                for h in range(8):
                    tr.op("dve", lambda e, h=h: e.scalar_tensor_tensor(out=kb[:, h, 0:64], in0=B[:, h * 128:h * 128 + 64], scalar=sc[:, h:h + 1],
                                                                       in1=Gkn[:, h, :], op0=ALU.mult, op1=ALU.mult),
                          reads=[r_B, r_sc, r_par], writes=[r_kb])
                    tr.op("dve", lambda e, h=h: e.scalar_tensor_tensor(out=qn[:, h, 64:96], in0=A[:, 0:32], scalar=hs[:, h:h + 1],
                                                                       in1=Gkr[:, h, :], op0=ALU.mult, op1=ALU.mult),
                          reads=[r_A, r_hs, r_par], writes=[r_qn])
                rope(tr, "dve", qn, r_qn, 64, 16, cmv, smv, r_tab, t1, r_t1, t2, r_t2, kb, r_kb)
                tr.op("act", lambda e: e.activation(out=v_st[:, s, :].rearrange("p (h d) -> p h d", h=8), in_=Bkv[:, :, 64:128], func=AF.Copy,
                                                    scale=rr[:, 1:2]),
                      reads=[r_B, r_rr], writes=[r_v_st])
                ti = tpi[0] % 2; tpi[0] += 1
                for h in range(8):
                    tr.op("pe", lambda e, h=h, ti=ti: e.transpose(out=tp[ti][0:96, h * 128:(h + 1) * 128], in_=kb[:, h, :], identity=ident[:]),
                          reads=[r_kb], writes=[r_tp[ti]])
                tr.op("act", lambda e, ti=ti: e.activation(out=kT_st[0:96, :, ssl], in_=tp[ti][0:96, :].rearrange("p (h t) -> p h t", h=8), func=AF.Copy),
                      reads=[r_tp[ti]], writes=[r_kT_st])
                if CUT < 7:
                    continue
                Agq = A[:, 32:544].rearrange("p (h d) -> p h d", h=8)
                tr.op("act", lambda e: e.activation(out=gqn[:], in_=Agq, func=AF.Square), reads=[r_A], writes=[r_gqn])
                tr.op("dve", lambda e: e.reduce_sum(out=hs[:], in_=gqn[:], axis=AX.X), reads=[r_gqn], writes=[r_hs])
                tr.op("act", lambda e: e.activation(out=hs[:], in_=hs[:], func=AF.Sqrt, scale=1.0 / 64, bias=EPS), reads=[r_hs], writes=[r_hs])
                tr.op("dve", lambda e: e.reciprocal(out=hs[:], in_=hs[:]), reads=[r_hs], writes=[r_hs])
                for h in range(8):
                    tr.op("dve", lambda e, h=h: e.scalar_tensor_tensor(out=gqn[:, h, :], in0=A[:, 32 + h * 64: 32 + (h + 1) * 64], scalar=hs[:, h:h + 1],
                                                                       in1=Ggq[:, h, :], op0=ALU.mult, op1=ALU.mult),
                          reads=[r_A, r_hs, r_par], writes=[r_gqn])
                cgv = cg[:, s, :].rearrange("p (h d) -> p h d", h=8)
                sgv = sg[:, s, :].rearrange("p (h d) -> p h d", h=8)
                rope(tr, "dve", gqn, r_gqn, 0, 32, cgv, sgv, r_tab, t1, r_t1, t2, r_t2, gqb, r_gqb)
                ti = tpi[0] % 2; tpi[0] += 1
                for j in range(4):
                    tr.op("pe", lambda e, j=j, ti=ti: e.transpose(out=tp[ti][:, j * 128:(j + 1) * 128],
                                                                  in_=gqb[:, 2 * j:2 * j + 2, :].rearrange("p h d -> p (h d)"), identity=ident[:]),
                          reads=[r_gqb], writes=[r_tp[ti]])
                tr.op("act", lambda e, ti=ti: e.activation(out=gqT_st[:, :, ssl], in_=tp[ti][:, 0:512].rearrange("p (h t) -> p h t", h=4), func=AF.Copy),
                      reads=[r_tp[ti]], writes=[r_gqT_st])
                Agk = A[:, 544:672].rearrange("p (h d) -> p h d", h=2)
                tr.op("act", lambda e: e.activation(out=gkn[:], in_=Agk, func=AF.Square), reads=[r_A], writes=[r_gkn])
                tr.op("dve", lambda e: e.reduce_sum(out=hs[:, 0:2], in_=gkn[:], axis=AX.X), reads=[r_gkn], writes=[r_hs])
                tr.op("act", lambda e: e.activation(out=hs[:, 0:2], in_=hs[:, 0:2], func=AF.Sqrt, scale=1.0 / 64, bias=EPS), reads=[r_hs], writes=[r_hs])
                tr.op("dve", lambda e: e.reciprocal(out=hs[:, 0:2], in_=hs[:, 0:2]), reads=[r_hs], writes=[r_hs])
                for h in range(2):
                    tr.op("dve", lambda e, h=h: e.scalar_tensor_tensor(out=gkn[:, h, :], in0=A[:, 544 + h * 64: 544 + (h + 1) * 64], scalar=hs[:, h:h + 1],
                                                                       in1=Ggk[:, h, :], op0=ALU.mult, op1=ALU.mult),
                          reads=[r_A, r_hs, r_par], writes=[r_gkn])
                rope(tr, "dve", gkn, r_gkn, 0, 32, cgv[:, 0:2, :], sgv[:, 0:2, :], r_tab, t1[:, 0:2, :], r_t1, t2[:, 0:2, :], r_t2, gkb, r_gkb)
                ti = tpi[0] % 2; tpi[0] += 1
                tr.op("pe", lambda e, ti=ti: e.transpose(out=tp[ti][:, 0:128], in_=gkb[:].rearrange("p h d -> p (h d)"), identity=ident[:]),
                      reads=[r_gkb], writes=[r_tp[ti]])
                tr.op("act", lambda e, ti=ti: e.activation(out=gkT_st[:, ssl], in_=tp[ti][:, 0:128], func=AF.Copy),
                      reads=[r_tp[ti]], writes=[r_gkT_st])
                tr.op("act", lambda e: e.activation(out=gv_st[:, s, :], in_=A[:, 672:800], func=AF.Copy), reads=[r_A], writes=[r_gv_st])
            if CUT < 8:
                continue
            tr.dma("sp", "st_q", qTm_d[:, :, tsl].rearrange("h d t -> d h t"), qT_st[0:96, :, :], reads=[r_qT_st])
            tr.dma("sp", "st_k", kTm_d[:, :, tsl].rearrange("h d t -> d h t"), kT_st[0:96, :, :], reads=[r_kT_st])
            tr.dma("sp", "st_gq", qTg_d[:, tsl].rearrange("(j p) t -> p j t", p=128), gqT_st[:], reads=[r_gqT_st])
            tr.dma("sp", "st_gk", kTg_d[:, tsl], gkT_st[:], reads=[r_gkT_st])
            tr.dma("sp", "st_v", vm_d[tsl, :].rearrange("(s p) f -> p s f", p=128), v_st[:], reads=[r_v_st])
            tr.dma("sp", "st_gv", vg_d[tsl, :].rearrange("(s p) f -> p s f", p=128), gv_st[:], reads=[r_gv_st])


R_const_dummy = Res()


def rope(tr, eng, src, r_src, off, half, cv, sv, r_tab, t1, r_t1, t2, r_t2, dst, r_dst):
    x1 = src[:, :, off:off + half]
    x2 = src[:, :, off + half:off + 2 * half]
    a = t1[:, :, 0:half]
    b2 = t2[:, :, 0:half]
    tr.op(eng, lambda e: e.tensor_tensor(out=a, in0=x1, in1=cv, op=ALU.mult), reads=[r_src, r_tab], writes=[r_t1])
    tr.op(eng, lambda e: e.tensor_tensor(out=b2, in0=x2, in1=sv, op=ALU.mult), reads=[r_src, r_tab], writes=[r_t2])
    tr.op(eng, lambda e: e.tensor_tensor(out=dst[:, :, off:off + half], in0=a, in1=b2, op=ALU.subtract), reads=[r_t1, r_t2], writes=[r_dst])
    tr.op(eng, lambda e: e.tensor_tensor(out=a, in0=x1, in1=sv, op=ALU.mult), reads=[r_src, r_tab], writes=[r_t1])
    tr.op(eng, lambda e: e.tensor_tensor(out=b2, in0=x2, in1=cv, op=ALU.mult), reads=[r_src, r_tab], writes=[r_t2])
    tr.op(eng, lambda e: e.tensor_tensor(out=dst[:, :, off + half:off + 2 * half], in0=a, in1=b2, op=ALU.add), reads=[r_t1, r_t2], writes=[r_dst])


def phase2(nc, tr, S, ident, ones_f, qTm_d, kTm_d, vm_d, qTg_d, kTg_d, vg_d, oT_d, rec_d):
    NK = S // 128
    NQ = S // 1024
    with contextlib.ExitStack() as ps:
        uid = "p2_%d_" % next(_UID)
        sb = lambda n, s, d: ps.enter_context(nc.sbuf_tensor(uid + n, list(s), d))
        pm = lambda n, s, d: ps.enter_context(nc.psum_tensor(uid + n, list(s), d))
        qt = [sb("qt%d" % i, [128, S], BF16) for i in range(2)]
        kt = [sb("kt%d" % i, [128, S], BF16) for i in range(2)]
        vt = [sb("vt%d" % i, [128, NK, 65], BF16) for i in range(2)]
        r_q = [Res(), Res()]; r_k = [Res(), Res()]; r_v = [Res(), Res()]
        pb = [sb("pb%d" % i, [128, 1024], BF16) for i in range(2)]; r_pb = [Res(), Res()]
        osb = sb("osb", [128, 1024], F32); r_osb = Res()
        rec = sb("rec", [128, 1024], F32); r_rec = Res()
        recb = sb("recb", [64, 1024], F32); r_recb = Res()
        r_recd = [Res(), Res()]
        on = [sb("on%d" % i, [64, 1024], BF16) for i in range(2)]; r_on = [Res(), Res()]
        Sp = [pm("S%d" % i, [128, 1024], F32) for i in range(2)]; r_S = [PRes(), PRes()]
        Ops = [pm("O%d" % i, [128, 1024], F32) for i in range(2)]; r_Os = [PRes(), PRes()]
        for i in range(2):
            tr.op("dve", lambda e, i=i: e.memset(vt[i][:, :, 64:65], 1.0), writes=[r_v[i]])
            tr.op("dve", lambda e, i=i: e.memset(qt[i][64:128, :], 0.0), writes=[r_q[i]])
            tr.op("dve", lambda e, i=i: e.memset(kt[i][64:128, :], 0.0), writes=[r_k[i]])

        def heads():
            for h in range(8):
                yield ("m", h, 96, qTm_d[h], kTm_d[h], vm_d[:, h * 64:(h + 1) * 64], h * 64)
            for j in range(8):
                g = j // 4
                yield ("g", j, 64, qTg_d[j * 64:(j + 1) * 64, :], kTg_d[g * 64:(g + 1) * 64, :], vg_d[:, g * 64:(g + 1) * 64], 512 + j * 64)

        hl = list(heads())

        def load(i):
            kind, h, dk, qa, ka, va, orow = hl[i]
            sl = i % 2
            if kind == "g" and h < 2:
                tr.op("dve", lambda e: e.memset(qt[sl][64:128, :], 0.0), writes=[r_q[sl]])
                tr.op("dve", lambda e: e.memset(kt[sl][64:128, :], 0.0), writes=[r_k[sl]])
            tr.dma("sp", "q%d" % sl, qt[sl][0:dk, :], qa, writes=[r_q[sl]])
            tr.dma("sp", "k%d" % sl, kt[sl][0:dk, :], ka, writes=[r_k[sl]])
            tr.dma("sp", "v%d" % sl, vt[sl][:, :, 0:64], va.rearrange("(k p) d -> p k d", p=128), writes=[r_v[sl]])

        load(0)
        it = 0
        oni = 0
        for i, (kind, h, dk, qa, ka, va, orow) in enumerate(hl):
            sl = i % 2
            if i + 1 < len(hl):
                load(i + 1)
            scale = float(dk) ** -0.5
            for qp in range(NQ):
                q0 = qp * 1024
                Op = Ops[oni % 2]; r_O = r_Os[oni % 2]

                def qk(ki, it):
                    sbi = it % 2
                    for half in range(2):
                        tr.op("pe", lambda e, half=half: e.matmul(Sp[sbi][:, half * 512:(half + 1) * 512],
                                                                 lhsT=kt[sl][:, ki * 128:(ki + 1) * 128],
                                                                 rhs=qt[sl][:, q0 + half * 512: q0 + (half + 1) * 512],
                                                                 start=True, stop=True),
                              reads=[r_q[sl], r_k[sl]], writes=[r_S[sbi]], embed=True)

                def ex(ki, it):
                    sbi = it % 2
                    tr.op("act", lambda e: e.activation(out=pb[sbi][:], in_=Sp[sbi][:], func=AF.Exp, scale=scale),
                          reads=[r_S[sbi]], writes=[r_pb[sbi]], skip_own=True, embed=True)

                def pv(ki, it):
                    sbi = it % 2
                    for half in range(2):
                        tr.op("pe", lambda e, half=half: e.matmul(Op[0:65, half * 512:(half + 1) * 512],
                                                                 lhsT=vt[sl][:, ki, 0:65],
                                                                 rhs=pb[sbi][:, half * 512:(half + 1) * 512],
                                                                 start=(ki == 0), stop=(ki == NK - 1)),
                              reads=[r_v[sl], r_pb[sbi]], writes=[r_O], embed=True)

                qk(0, it)
                for ki in range(NK):
                    if ki + 1 < NK:
                        qk(ki + 1, it + ki + 1)
                    ex(ki, it + ki)
                    pv(ki, it + ki)
                it += NK
                tr.op("dve", lambda e: e.tensor_copy(out=osb[0:65, :], in_=Op[0:65, :]), reads=[r_O], writes=[r_osb])
                tr.op("dve", lambda e: e.reciprocal(out=rec[64:65, :], in_=osb[64:65, :]), reads=[r_osb], writes=[r_rec])
                oi = oni % 2; oni += 1
                tr.dma("sp", "recw%d" % oi, rec_d[oi:oi + 1, :], rec[64:65, :], reads=[r_rec], writes=[r_recd[oi]])
                tr.dma("sp", "recr", recb[:], bass.AP(rec_d.tensor, oi * 1024, [[0, 64], [1, 1024]]), reads=[r_recd[oi]], writes=[r_recb])
                tr.op("dve", lambda e, oi=oi: e.tensor_tensor(out=on[oi][:], in0=osb[0:64, :], in1=recb[:], op=ALU.mult),
                      reads=[r_osb, r_recb], writes=[r_on[oi]])
                tr.dma("sp", "on%d" % oi, oT_d[orow:orow + 64, q0:q0 + 1024], on[oi][:], reads=[r_on[oi]])


def phase3(nc, tr, l, S, x_src, out, w_in, pool_w, w_bm, w_bg, w_bp, w_out, p_pscale, p_ffn_g, c_invc, ident, hT_d, pT_d, oT_d):
    TT = 512
    NT = S // TT
    NS = TT // 128
    W = TT + 16
    with contextlib.ExitStack() as ps:
        uid = "p3_%d_" % next(_UID)
        sb = lambda n, s, d: ps.enter_context(nc.sbuf_tensor(uid + n, list(s), d))
        pm = lambda n, s, d: ps.enter_context(nc.psum_tensor(uid + n, list(s), d))
        wg = sb("wg", [128, 8, 3072], BF16); r_wg = Res()
        wb = [sb("wb%d" % i, [128, 4, 1024], BF16) for i in range(3)]; r_wb = [Res(), Res(), Res()]
        wo = sb("wo", [128, 8, 1024], BF16); r_wo = Res()
        wp = sb("wp", [128, 4, 128], BF16); r_wp = Res()
        with contextlib.ExitStack() as ps2:
            stg = [ps2.enter_context(nc.sbuf_tensor(uid + "stg%d" % i, [128, 2048], F32)) for i in range(2)]
            stg_res = [Res(), Res()]
            st = [0]
            load_cast(nc, tr, ps, "wg", wg, r_wg, w_in[l][:, 1696:4768], 3072, 8, stg, stg_res, st)
            for i, wsrc in enumerate((w_bm, w_bg, w_bp)):
                load_cast(nc, tr, ps, "wb", wb[i], r_wb[i], wsrc[l], 1024, 4, stg, stg_res, st)
            load_cast(nc, tr, ps, "wo", wo, r_wo, w_out[l], 1024, 8, stg, stg_res, st)
            load_cast(nc, tr, ps, "wp", wp, r_wp, pool_w[l].rearrange("g c d -> (g c) d"), 128, 4, stg, stg_res, st)
            tr.barrier()
        pscale = sb("pscale", [128, 4], F32); g_ffn = sb("g_ffn", [128, 8], F32); r_par = Res()
        tr.dma("sp", "par", pscale[:], p_pscale[l], writes=[r_par])
        tr.dma("sp", "par", g_ffn[:], p_ffn_g[l], writes=[r_par])

        hT = [sb("hT%d" % i, [128, 8, TT], BF16) for i in range(2)]; r_hT = [Res(), Res()]
        oT = [sb("oT%d" % i, [128, 8, TT], BF16) for i in range(2)]; r_oT = [Res(), Res()]
        pin = sb("pin", [128, 4, W], F32); r_pin = Res()
        ivc = sb("ivc", [128, 4, TT], F32); r_ivc = Res()
        xs = [sb("xs%d" % i, [128, 1024], F32) for i in range(NS)]; r_xs = [Res() for _ in range(NS)]
        tA = sb("tA", [128, W], F32); r_tA = Res()
        tB = sb("tB", [128, W], F32); r_tB = Res()
        mxg = sb("mxg", [128, TT], F32); r_mxg = Res()
        mxb = [sb("mxb%d" % i, [128, 4, TT], BF16) for i in range(2)]; r_mxb = [Res(), Res()]
        opl = sb("opl", [128, 4, TT], BF16); r_opl = Res()
        sg = [sb("sg%d" % i, [128, TT], F32) for i in range(3)]; r_sg = [Res(), Res(), Res()]
        mm = sb("mm", [128, TT], F32); r_mm = Res()
        mg = sb("mg", [128, 8, TT], BF16); r_mg = Res()
        junk = sb("junk", [128, 1024], F32); r_junk = Res()
        ss = sb("ss", [128, NS], F32); r_ss = Res()
        rstd = sb("rstd", [128, NS], F32); r_rstd = Res()
        xn = sb("xn", [128, NS, 1024], BF16); r_xn = Res()
        Y = [pm("Y%d" % i, [128, 512], F32) for i in range(3)]; r_Y = [PRes(), PRes(), PRes()]
        Gp = [pm("G%d" % i, [128, 512], F32) for i in range(3)]; r_G = [PRes(), PRes(), PRes()]
        M0 = pm("M0", [128, 512], F32); r_M0 = PRes()
        M1 = pm("M1", [128, 512], F32); r_M1 = PRes()
        tpb = M1[:].bitcast(BF16)

        def load_a(T):
            b = T % 2
            t0 = T * TT
            tr.dma("sp", "h%d" % b, hT[b][:], hT_d[:, t0:t0 + TT].rearrange("(c p) t -> p c t", p=128), writes=[r_hT[b]])
            tr.dma("sp", "o%d" % b, oT[b][:], oT_d[:, t0:t0 + TT].rearrange("(c p) t -> p c t", p=128), writes=[r_oT[b]])

        def load_b(T):
            t0 = T * TT
            lo = max(t0 - 8, 0); hi = min(t0 + TT + 8, S)
            if lo != t0 - 8:
                tr.op("dve", lambda e: e.memset(pin[:, :, 0:8], 0.0), writes=[r_pin])
            if hi != t0 + TT + 8:
                tr.op("dve", lambda e: e.memset(pin[:, :, TT + 8:TT + 16], 0.0), writes=[r_pin])
            tr.dma("sp", "p0", pin[:, :, lo - (t0 - 8): hi - (t0 - 8)], pT_d[:, lo:hi].rearrange("(g p) t -> p g t", p=128), writes=[r_pin])
            tr.dma("sp", "i0", ivc[:], bass.AP(c_invc.tensor, t0, [[0, 128], [S, 4], [1, TT]]), writes=[r_ivc])

        def pool_chain(T, g):
            P = pin[:, g, :]
            tr.op("dve", lambda e, P=P: e.tensor_tensor(out=tA[:, 0:W - 1], in0=P[:, 0:W - 1], in1=P[:, 1:W], op=ALU.add),
                  reads=[r_pin], writes=[r_tA])
            src, rsrc, j0 = tA, r_tA, 7
            if g >= 1:
                tr.op("dve", lambda e: e.tensor_tensor(out=tB[:, 0:W - 3], in0=tA[:, 0:W - 3], in1=tA[:, 2:W - 1], op=ALU.add),
                      reads=[r_tA], writes=[r_tB])
                src, rsrc, j0 = tB, r_tB, 6
            if g >= 2:
                tr.op("dve", lambda e: e.tensor_tensor(out=tA[:, 0:W - 7], in0=tB[:, 0:W - 7], in1=tB[:, 4:W - 3], op=ALU.add),
                      reads=[r_tB], writes=[r_tA])
                src, rsrc, j0 = tA, r_tA, 4
            if g >= 3:
                tr.op("dve", lambda e: e.tensor_tensor(out=tB[:, 0:W - 15], in0=tA[:, 0:W - 15], in1=tA[:, 8:W - 7], op=ALU.add),
                      reads=[r_tA], writes=[r_tB])
                src, rsrc, j0 = tB, r_tB, 0
            tr.op("dve", lambda e, g=g, src=src, j0=j0: e.tensor_tensor(out=mxg[:], in0=src[:, j0:j0 + TT], in1=ivc[:, g, :], op=ALU.mult),
                  reads=[rsrc, r_ivc], writes=[r_mxg])
            tr.op("dve", lambda e, g=g, P=P: e.tensor_tensor(out=mxb[T % 2][:, g, :], in0=mxg[:], in1=P[:, 8:8 + TT], op=ALU.subtract),
                  reads=[r_mxg, r_pin], writes=[r_mxb[T % 2]])

        load_a(0)
        load_b(0)
        for g in range(4):
            pool_chain(0, g)
        if NT > 1:
            load_b(1)
        for T in range(NT):
            b = T % 2
            t0 = T * TT
            if T + 1 < NT:
                load_a(T + 1)
            for s in range(NS):
                tr.dma("sp", "x%d" % s, xs[s][:], x_src[t0 + s * 128: t0 + (s + 1) * 128, :], writes=[r_xs[s]])
            for g in range(4):
                tr.op("pe", lambda e, g=g: e.matmul(M0[:, 0:TT], lhsT=wp[:, g, :], rhs=mxb[b][:, g, :], start=True, stop=True),
                      reads=[r_wp, r_mxb[b]], writes=[r_M0])
                tr.op("act", lambda e, g=g: e.activation(out=opl[:, g, :], in_=M0[:, 0:TT], func=AF.Copy, scale=pscale[:, g:g + 1]),
                      reads=[r_M0, r_par], writes=[r_opl])
            srcs = ((oT[b], r_oT[b], 0), (oT[b], r_oT[b], 4), (opl, r_opl, 0))
            for n in range(8):
                nsl = slice(n * 128, (n + 1) * 128)
                for br in range(3):
                    for c in range(8):
                        tr.op("pe", lambda e, br=br, c=c: e.matmul(Gp[br][:, 0:TT], lhsT=wg[:, c, br * 1024 + n * 128: br * 1024 + (n + 1) * 128],
                                                                  rhs=hT[b][:, c, :], start=(c == 0), stop=(c == 7)),
                              reads=[r_wg, r_hT[b]], writes=[r_G[br]])
                    tr.op("act", lambda e, br=br: e.activation(out=sg[br][:], in_=Gp[br][:, 0:TT], func=AF.Sigmoid),
                          reads=[r_G[br]], writes=[r_sg[br]])
                for br, (src, rsrc, coff) in enumerate(srcs):
                    for c in range(4):
                        tr.op("pe", lambda e, br=br, c=c, src=src, coff=coff: e.matmul(Y[br][:, 0:TT], lhsT=wb[br][:, c, nsl], rhs=src[:, coff + c, :],
                                                                                      start=(c == 0), stop=(c == 3)),
                              reads=[r_wb[br], rsrc], writes=[r_Y[br]])
                tr.op("dve", lambda e: e.tensor_tensor(out=mm[:], in0=sg[0][:], in1=Y[0][:, 0:TT], op=ALU.mult),
                      reads=[r_sg[0], r_Y[0]], writes=[r_mm])
                tr.op("dve", lambda e: e.tensor_tensor(out=sg[1][:], in0=sg[1][:], in1=Y[1][:, 0:TT], op=ALU.mult),
                      reads=[r_sg[1], r_Y[1]], writes=[r_sg[1]])
                tr.op("dve", lambda e: e.tensor_tensor(out=sg[2][:], in0=sg[2][:], in1=Y[2][:, 0:TT], op=ALU.mult),
                      reads=[r_sg[2], r_Y[2]], writes=[r_sg[2]])
                tr.op("dve", lambda e: e.tensor_tensor(out=mm[:], in0=mm[:], in1=sg[1][:], op=ALU.add), reads=[r_mm, r_sg[1]], writes=[r_mm])
                tr.op("dve", lambda e, n=n: e.tensor_tensor(out=mg[:, n, :], in0=mm[:], in1=sg[2][:], op=ALU.add), reads=[r_mm, r_sg[2]], writes=[r_mg])
                if n < 4 and T + 1 < NT:
                    pool_chain(T + 1, n)
                    if n == 3 and T + 2 < NT:
                        load_b(T + 2)
            for s in range(NS):
                for half in range(2):
                    Mx, rMx = (M0, r_M0) if half == 0 else (M1, r_M1)
                    for c in range(8):
                        tr.op("pe", lambda e, c=c, Mx=Mx, half=half, s=s: e.matmul(Mx[:], lhsT=mg[:, c, s * 128:(s + 1) * 128], rhs=wo[:, c, half * 512:(half + 1) * 512],
                                                                                  start=(c == 0), stop=(c == 7)),
                              reads=[r_wo, r_mg], writes=[rMx])
                    tr.op("dve", lambda e, Mx=Mx, half=half, s=s: e.tensor_tensor(out=xs[s][:, half * 512:(half + 1) * 512], in0=xs[s][:, half * 512:(half + 1) * 512],
                                                                                  in1=Mx[:], op=ALU.add),
                          reads=[rMx, r_xs[s]], writes=[r_xs[s]])
                tr.dma("sp", "xo%d" % s, out[t0 + s * 128: t0 + (s + 1) * 128, :], xs[s][:], reads=[r_xs[s]])
                tr.op("act", lambda e, s=s: e.activation(out=junk[:], in_=xs[s][:], func=AF.Square, accum_out=ss[:, s:s + 1]),
                      reads=[r_xs[s]], writes=[r_junk, r_ss])
            tr.op("act", lambda e: e.activation(out=rstd[:], in_=ss[:], func=AF.Sqrt, scale=1.0 / D, bias=EPS), reads=[r_ss], writes=[r_rstd])
            tr.op("dve", lambda e: e.reciprocal(out=rstd[:], in_=rstd[:]), reads=[r_rstd], writes=[r_rstd])
            for s in range(NS):
                tr.op("dve", lambda e, s=s: e.tensor_scalar(out=xn[:, s, :], in0=xs[s][:], scalar1=rstd[:, s:s + 1], scalar2=None, op0=ALU.mult),
                      reads=[r_xs[s], r_rstd], writes=[r_xn])
            for r in range(4):
                for cc in range(2):
                    c = 2 * r + cc
                    for s in range(NS):
                        tr.op("pe", lambda e, c=c, cc=cc, s=s: e.transpose(out=tpb[:, cc * TT + s * 128: cc * TT + (s + 1) * 128],
                                                                           in_=xn[:, s, c * 128:(c + 1) * 128], identity=ident[:]),
                              reads=[r_xn], writes=[r_M1])
                for cc in range(2):
                    c = 2 * r + cc
                    if cc % 2 == 0:
                        tr.op("act", lambda e, c=c, cc=cc: e.activation(out=hT[b][:, c, :], in_=tpb[:, cc * TT:(cc + 1) * TT], func=AF.Copy, scale=g_ffn[:, c:c + 1]),
                              reads=[r_M1, r_par], writes=[r_hT[b]])
                    else:
                        tr.op("dve", lambda e, c=c, cc=cc: e.tensor_scalar(out=hT[b][:, c, :], in0=tpb[:, cc * TT:(cc + 1) * TT], scalar1=g_ffn[:, c:c + 1],
                                                                           scalar2=None, op0=ALU.mult),
                              reads=[r_M1, r_par], writes=[r_hT[b]])
            tr.dma("sp", "hf%d" % b, hT_d[:, t0:t0 + TT].rearrange("(c p) t -> p c t", p=128), hT[b][:], reads=[r_hT[b]])


def phase4(nc, tr, l, S, out, w_up, w_down, p_convw, p_convb, hT_d):
    TT = 512
    NT = S // TT
    NS = TT // 128
    with contextlib.ExitStack() as ps:
        uid = "p4_%d_" % next(_UID)
        sb = lambda n, s, d: ps.enter_context(nc.sbuf_tensor(uid + n, list(s), d))
        pm = lambda n, s, d: ps.enter_context(nc.psum_tensor(uid + n, list(s), d))
        wu = sb("wu", [128, 8, 2 * DFF], BF16); r_wu = Res()
        wd = sb("wd", [128, 22, 1024], BF16); r_wd = Res()
        with contextlib.ExitStack() as ps2:
            stg = [ps2.enter_context(nc.sbuf_tensor(uid + "stg%d" % i, [128, 2816], F32)) for i in range(2)]
            stg_res = [Res(), Res()]
            st = [0]
            load_cast(nc, tr, ps2, "wu", wu, r_wu, w_up[l], 2 * DFF, 8, stg, stg_res, st)
            load_cast(nc, tr, ps2, "wd", wd, r_wd, w_down[l], 1024, 22, stg, stg_res, st)
            tr.barrier()
        cw = sb("cw", [128, 3, 44], F32); cb = sb("cb", [128, 44], F32); r_par = Res()
        tr.dma("sp", "par", cw[:], p_convw[l], writes=[r_par])
        tr.dma("sp", "par", cb[:], p_convb[l], writes=[r_par])
        hf = [sb("hf%d" % i, [128, 8, TT + 2], BF16) for i in range(2)]; r_hf = [Res(), Res()]
        xs = [sb("xs%d" % i, [128, 1024], F32) for i in range(NS)]; r_xs = [Res() for _ in range(NS)]
        aT = sb("aT", [128, 22, TT], BF16); r_aT = Res()
        y0 = [sb("y0_%d" % i, [128, TT], F32) for i in range(2)]; r_y0 = [Res(), Res()]
        y1 = [sb("y1_%d" % i, [128, TT], F32) for i in range(2)]; r_y1 = [Res(), Res()]
        U = [pm("U%d" % i, [128, 1024], F32) for i in range(3)]; r_U = [PRes() for _ in range(3)]
        Dn = [pm("Dn%d" % i, [128, 512], F32) for i in range(2)]; r_Dn = [PRes() for _ in range(2)]

        def load(T):
            b = T % 2
            t0 = T * TT
            lo = max(t0 - 1, 0); hi = min(t0 + TT + 1, S)
            if lo != t0 - 1:
                tr.op("dve", lambda e: e.memset(hf[b][:, :, 0:1], 0.0), writes=[r_hf[b]])
            if hi != t0 + TT + 1:
                tr.op("dve", lambda e: e.memset(hf[b][:, :, TT + 1:TT + 2], 0.0), writes=[r_hf[b]])
            tr.dma("sp", "h%d" % b, hf[b][:, :, lo - (t0 - 1): hi - (t0 - 1)], hT_d[:, lo:hi].rearrange("(c p) t -> p c t", p=128), writes=[r_hf[b]])

        load(0)
        ui = 0
        di = 0
        for T in range(NT):
            b = T % 2
            t0 = T * TT
            if T + 1 < NT:
                load(T + 1)
            for s in range(NS):
                tr.dma("sp", "x%d" % s, xs[s][:], out[t0 + s * 128: t0 + (s + 1) * 128, :], writes=[r_xs[s]])
            for pr in range(22):
                for which, cf in ((0, pr), (1, pr + 22)):
                    u = ui % 3; ui += 1
                    for (c0, c1) in ((0, 512), (512, TT + 2)):
                        for c in range(8):
                            tr.op("pe", lambda e, c=c, u=u, cf=cf, c0=c0, c1=c1: e.matmul(U[u][:, c0:c1], lhsT=wu[:, c, cf * 128:(cf + 1) * 128], rhs=hf[b][:, c, c0:c1],
                                                                                       start=(c == 0), stop=(c == 7)),
                                  reads=[r_wu, r_hf[b]], writes=[r_U[u]])
                    tr.op("act", lambda e, u=u, cf=cf, which=which: e.activation(out=y0[which][:], in_=U[u][:, 1:TT + 1], func=AF.Identity,
                                                                                 scale=cw[:, 1, cf:cf + 1], bias=cb[:, cf:cf + 1]),
                          reads=[r_U[u], r_par], writes=[r_y0[which]])
                    tr.op("dve", lambda e, u=u, cf=cf, which=which: e.scalar_tensor_tensor(out=y1[which][:], in0=U[u][:, 0:TT], scalar=cw[:, 0, cf:cf + 1],
                                                                                           in1=y0[which][:], op0=ALU.mult, op1=ALU.add),
                          reads=[r_U[u], r_par, r_y0[which]], writes=[r_y1[which]])
                    tr.op("dve", lambda e, u=u, cf=cf, which=which: e.scalar_tensor_tensor(out=y0[which][:], in0=U[u][:, 2:TT + 2], scalar=cw[:, 2, cf:cf + 1],
                                                                                           in1=y1[which][:], op0=ALU.mult, op1=ALU.add),
                          reads=[r_U[u], r_par, r_y1[which]], writes=[r_y0[which]])
                tr.op("act", lambda e: e.activation(out=y1[0][:], in_=y0[0][:], func=AF.Silu), reads=[r_y0[0]], writes=[r_y1[0]])
                tr.op("dve", lambda e, pr=pr: e.tensor_tensor(out=aT[:, pr, :], in0=y1[0][:], in1=y0[1][:], op=ALU.mult),
                      reads=[r_y1[0], r_y0[1]], writes=[r_aT])
            for s in range(NS):
                for half in range(2):
                    d = di % 2; di += 1
                    for c in range(22):
                        tr.op("pe", lambda e, c=c, d=d, half=half: e.matmul(Dn[d][:], lhsT=aT[:, c, s * 128:(s + 1) * 128], rhs=wd[:, c, half * 512:(half + 1) * 512],
                                                                           start=(c == 0), stop=(c == 21)),
                              reads=[r_wd, r_aT], writes=[r_Dn[d]])
                    tr.op("dve", lambda e, d=d, half=half, s=s: e.tensor_tensor(out=xs[s][:, half * 512:(half + 1) * 512], in0=xs[s][:, half * 512:(half + 1) * 512],
                                                                                in1=Dn[d][:], op=ALU.add),
                          reads=[r_Dn[d], r_xs[s]], writes=[r_xs[s]])
                tr.dma("sp", "xo%d" % s, out[t0 + s * 128: t0 + (s + 1) * 128, :], xs[s][:], reads=[r_xs[s]])


_NC_CACHE = {}


def make_in_maps(inputs, S, L, ncores):
    consts = host_consts(S)
    lay = host_layout(inputs, L)
    f = lambda a: np.ascontiguousarray(a, dtype=np.float32)
    shared = {
        "w_in": f(inputs["w_in"]), "mla_w_uq": f(inputs["mla_w_uq"]), "mla_w_ukv": f(inputs["mla_w_ukv"]),
        "pool_w": f(inputs["pool_w"]), "w_branch_mla": f(inputs["w_branch_mla"]), "w_branch_gqa": f(inputs["w_branch_gqa"]),
        "w_branch_pool": f(inputs["w_branch_pool"]), "w_out": f(inputs["w_out"]), "ffn_w_up": f(inputs["ffn_w_up"]),
        "ffn_w_down": f(inputs["ffn_w_down"]),
    }
    shared.update(lay)
    shared.update(consts)
    x = np.asarray(inputs["x"], dtype=np.float32)
    maps = []
    for c in range(ncores):
        m = dict(shared)
        m["x"] = np.ascontiguousarray(x[c])
        maps.append(m)
    return maps


def kernel(**inputs):
    x = np.asarray(inputs["x"])
    B, S, _ = x.shape
    L = int(np.asarray(inputs["w_in"]).shape[0])
    key = (S, L)
    if key not in _NC_CACHE:
        _NC_CACHE[key] = build_nc(S, L)
    nc = _NC_CACHE[key]
    maps = make_in_maps(inputs, S, L, B)
    res = run_bass_kernel_spmd(nc, maps, core_ids=list(range(B)))
    return np.stack([np.asarray(res.results[c]["out"], dtype=np.float32) for c in range(B)], axis=0)
```

```python
import contextlib
import itertools
_UID = itertools.count()
import os
CUT = int(os.environ.get('K_CUT', '99'))
import numpy as np
import concourse.bass as bass
import concourse.mybir as mybir
from concourse.bass_utils import run_bass_kernel_spmd

F32 = mybir.dt.float32
BF16 = mybir.dt.bfloat16
AF = mybir.ActivationFunctionType
ALU = mybir.AluOpType
AX = mybir.AxisListType

D = 1024
DEPTH = 2
SEQ = 8192
NCORES = 8
EPS = 1e-6
IN_W = 4768
DFF = 2816
GRID_W = 64
ROPE_THETA = 10000.0
POOL_WINDOWS = (2, 4, 8, 16)
WARM_N = 0


class Res:
    __slots__ = ("w", "r", "name", "psum")

    def __init__(self, name="", psum=False):
        self.w = None
        self.r = {}
        self.name = name
        self.psum = psum


def PRes():
    return Res(psum=True)


class TR:
    def __init__(self, nc, es):
        self.nc = nc
        self.es = es
        self.eng = {"pe": nc.tensor, "act": nc.scalar, "dve": nc.vector, "pool": nc.gpsimd, "sp": (nc.gpsimd if os.environ.get("K_SWDGE") else nc.sync)}
        self.sems = {}
        self.cnt = {}
        self.waited = {e: {} for e in self.eng}
        self.gen = 0
        self.pending = None
        self.ekey = {}
        self._new_engine_sems()

    def _new_engine_sems(self):
        self.gen += 1
        for e in self.eng:
            key = "e%d_%s" % (self.gen, e)
            self.sems[key] = self.es.enter_context(self.nc.semaphore(key))
            self.cnt[key] = 0
            self.ekey[e] = key

    def dsem(self, name):
        key = "d_" + name
        if key not in self.sems:
            self.sems[key] = self.es.enter_context(self.nc.semaphore(key))
            self.cnt[key] = 0
        return key

    def _wait(self, e, key, val):
        if val <= 0 or self.waited[e].get(key, 0) >= val:
            return
        if key == self.ekey[e] and (e == "pe" or os.environ.get("K_NOOWN")):
            return
        if self.pending is not None:
            self.pending = [(k, v) for (k, v) in self.pending if k != key] + [(key, val)]
            self.waited[e][key] = val
            return
        self.eng[e].wait_ge(self.sems[key], val)
        self.waited[e][key] = val

    def _deps(self, e, reads, writes, skip_own=False):
        if skip_own:
            own = self.ekey.get(e)
            sv = self.waited[e].get(own, 0)
            self.waited[e][own] = 1 << 60
            try:
                self._deps(e, reads, writes)
            finally:
                self.waited[e][own] = sv
            return
        for r in reads:
            if r.w:
                self._wait(e, *r.w)
            if r.psum:
                own = self.ekey.get(e)
                for k, v in r.r.items():
                    if k != own:
                        self._wait(e, k, v)
        for r in writes:
            if r.w:
                self._wait(e, *r.w)
            for k, v in r.r.items():
                self._wait(e, k, v)

    def _mark(self, key, v, reads, writes):
        for r in reads:
            if r.r.get(key, 0) < v:
                r.r[key] = v
        for r in writes:
            r.w = (key, v)
            r.r = {}

    def op(self, e, fn, reads=(), writes=(), skip_own=False, embed=True):
        if embed:
            self.pending = []
        self._deps(e, reads, writes, skip_own)
        emb = None
        if embed:
            pend, self.pending = self.pending, None
            for k, v in pend[:-1]:
                self.eng[e].wait_ge(self.sems[k], v)
            if pend:
                emb = pend[-1]
        ins = fn(self.eng[e])
        if emb is not None:
            ins._wait_ge(self.sems[emb[0]], emb[1])
        key = self.ekey[e]
        self.cnt[key] += 1
        ins.then_inc(self.sems[key], 1)
        self._mark(key, self.cnt[key], reads, writes)
        return ins

    def dma(self, q, sem, out, in_, reads=(), writes=(), **kw):
        self._deps(q, reads, writes)
        key = self.dsem(sem)
        ins = self.eng[q].dma_start(out=out, in_=in_, **kw)
        self.cnt[key] += 16
        ins.then_inc(self.sems[key], 16)
        self._mark(key, self.cnt[key], reads, writes)
        return ins

    def barrier(self):
        for e in self.eng:
            for key, c in self.cnt.items():
                self._wait(e, key, c)
        if max(self.cnt[k] for k in self.ekey.values()) > 100000:
            self._new_engine_sems()


def host_consts(S):
    rows = S // GRID_W
    row_idx = np.repeat(np.arange(rows, dtype=np.float32), GRID_W)
    col_idx = np.tile(np.arange(GRID_W, dtype=np.float32), rows)

    def tables(rot_dim):
        n_axis = rot_dim // 4
        inv_freq = (np.float32(ROPE_THETA) ** (-np.arange(n_axis, dtype=np.float32) / np.float32(n_axis))).astype(np.float32)
        ang = np.concatenate([row_idx[:, None] * inv_freq, col_idx[:, None] * inv_freq], axis=-1).astype(np.float32)
        return np.cos(ang).astype(np.float32), np.sin(ang).astype(np.float32)

    cm, sm = tables(32)
    cg, sg = tables(64)
    t = np.arange(S)
    invc = np.zeros((4, S), np.float32)
    for gi, w in enumerate(POOL_WINDOWS):
        lo = np.clip(t - w // 2, 0, S)
        hi = np.clip(t + w - w // 2, 0, S)
        invc[gi] = (1.0 / (hi - lo).astype(np.float32)).astype(np.float32)
    return {
        "c_ident": np.eye(128, dtype=np.float32),
        "c_cm": np.ascontiguousarray(np.tile(cm, (1, 8))),
        "c_sm": np.ascontiguousarray(np.tile(sm, (1, 8))),
        "c_cg": np.ascontiguousarray(np.tile(cg, (1, 8))),
        "c_sg": np.ascontiguousarray(np.tile(sg, (1, 8))),
        "c_invc": np.ascontiguousarray(np.broadcast_to(invc[None], (128, 4, S))) if False else invc,
    }


def host_layout(inp, l_count):
    o = {}
    f = lambda a: np.ascontiguousarray(a, dtype=np.float32)
    L = l_count
    o["p_attn_g"] = f(inp["attn_norm_g"].reshape(L, 8, 128).transpose(0, 2, 1))
    o["p_ffn_g"] = f(inp["ffn_norm_g"].reshape(L, 8, 128).transpose(0, 2, 1))
    o["p_gq"] = f(inp["mla_q_norm_g"].reshape(L, 2, 128).transpose(0, 2, 1))
    o["p_gkv"] = f(inp["mla_kv_norm_g"].reshape(L, 1, 128).transpose(0, 2, 1))
    o["p_pscale"] = f(inp["pool_scale"].reshape(L, 4, 128).transpose(0, 2, 1))
    o["p_convw"] = f(inp["ffn_conv_w"].reshape(L, 3, 44, 128).transpose(0, 3, 1, 2))
    o["p_convb"] = f(inp["ffn_conv_b"].reshape(L, 44, 128).transpose(0, 2, 1))
    rep = lambda v, n: np.broadcast_to(np.tile(v, (1, n))[:, None, :], (L, 128, v.shape[1] * n))
    o["p_Gq"] = f(rep(inp["mla_q_head_g"], 8))
    o["p_Gkn"] = f(rep(inp["mla_k_head_g"][:, :64], 8))
    o["p_Gkr"] = f(rep(inp["mla_k_head_g"][:, 64:], 8))
    o["p_Ggq"] = f(rep(inp["gqa_q_head_g"], 8))
    o["p_Ggk"] = f(rep(inp["gqa_k_head_g"], 2))
    return o


def build_nc(S=SEQ, L=DEPTH, debug_stop=None):
    assert S % 1024 == 0
    nc = bass.Bass("TRN2", target_bir_lowering=False)
    NT = S // 512

    def din(name, shape, dt=F32):
        return nc.dram_tensor(name, list(shape), dt, kind="ExternalInput").ap()

    def dout(name, shape, dt=F32):
        return nc.dram_tensor(name, list(shape), dt, kind="ExternalOutput").ap()

    x_in = din("x", [S, D])
    w_in = din("w_in", [L, D, IN_W])
    w_uq = din("mla_w_uq", [L, 256, 768])
    w_ukv = din("mla_w_ukv", [L, 128, 1024])
    pool_w = din("pool_w", [L, 4, 128, 128])
    w_bm = din("w_branch_mla", [L, 512, D])
    w_bg = din("w_branch_gqa", [L, 512, D])
    w_bp = din("w_branch_pool", [L, 512, D])
    w_out = din("w_out", [L, D, D])
    w_up = din("ffn_w_up", [L, D, 2 * DFF])
    w_down = din("ffn_w_down", [L, DFF, D])
    p_attn_g = din("p_attn_g", [L, 128, 8])
    p_ffn_g = din("p_ffn_g", [L, 128, 8])
    p_gq = din("p_gq", [L, 128, 2])
    p_gkv = din("p_gkv", [L, 128, 1])
    p_pscale = din("p_pscale", [L, 128, 4])
    p_convw = din("p_convw", [L, 128, 3, 44])
    p_convb = din("p_convb", [L, 128, 44])
    p_Gq = din("p_Gq", [L, 128, 768])
    p_Gkn = din("p_Gkn", [L, 128, 512])
    p_Gkr = din("p_Gkr", [L, 128, 256])
    p_Ggq = din("p_Ggq", [L, 128, 512])
    p_Ggk = din("p_Ggk", [L, 128, 128])
    c_ident = din("c_ident", [128, 128])
    c_cm = din("c_cm", [S, 128])
    c_sm = din("c_sm", [S, 128])
    c_cg = din("c_cg", [S, 256])
    c_sg = din("c_sg", [S, 256])
    c_invc = din("c_invc", [4, S])

    out = dout("out", [S, D])
    hT_d = dout("s_hT", [D, S], BF16)
    qTm_d = dout("s_qTm", [8, 96, S], BF16)
    kTm_d = dout("s_kTm", [8, 96, S], BF16)
    vm_d = dout("s_vm", [S, 512], BF16)
    qTg_d = dout("s_qTg", [512, S], BF16)
    kTg_d = dout("s_kTg", [128, S], BF16)
    vg_d = dout("s_vg", [S, 128], BF16)
    pT_d = dout("s_pT", [512, S], F32)
    oT_d = dout("s_oT", [1024, S], BF16)
    rec_d = dout("s_rec", [2, 1024], F32)

    es = contextlib.ExitStack()
    with es:
        tr = TR(nc, es)
        sbg = lambda n, s, d: es.enter_context(nc.sbuf_tensor(n, list(s), d))
        ident = sbg("ident", [128, 128], BF16)
        ones_f = sbg("ones_f", [128, 128], F32)
        zeros = sbg("zeros", [128, 16], F32)
        R_const = Res("const")

        with contextlib.ExitStack() as ps:
            identf = ps.enter_context(nc.sbuf_tensor("identf", [128, 128], F32))
            r_idf = Res()
            tr.dma("sp", "c0", identf[:], c_ident, writes=[r_idf])
            tr.op("dve", lambda e: e.tensor_copy(out=ident[:], in_=identf[:]), reads=[r_idf], writes=[R_const])
            tr.op("dve", lambda e: e.memset(ones_f[:], 1.0), writes=[R_const])
            tr.op("dve", lambda e: e.memset(zeros[:], 0.0), writes=[R_const])
            tr.barrier()

        for l in range(L):
            x_src = x_in if l == 0 else out
            phase1(nc, tr, l, S, x_src, w_in, w_uq, w_ukv, p_attn_g, p_gq, p_gkv, p_Gq, p_Gkn, p_Gkr, p_Ggq, p_Ggk,
                   c_cm, c_sm, c_cg, c_sg, ident, ones_f, hT_d, qTm_d, kTm_d, vm_d, qTg_d, kTg_d, vg_d, pT_d)
            tr.barrier()
            if debug_stop == (l, 1):
                break
            phase2(nc, tr, S, ident, ones_f, qTm_d, kTm_d, vm_d, qTg_d, kTg_d, vg_d, oT_d, rec_d)
            tr.barrier()
            if debug_stop == (l, 2):
                break
            phase3(nc, tr, l, S, x_src, out, w_in, pool_w, w_bm, w_bg, w_bp, w_out, p_pscale, p_ffn_g, c_invc,
                   ident, hT_d, pT_d, oT_d)
            tr.barrier()
            if debug_stop == (l, 3):
                break
            phase4(nc, tr, l, S, out, w_up, w_down, p_convw, p_convb, hT_d)
            tr.barrier()
    return nc


def load_cast(nc, tr, ps, name, dst, dst_res, src_rows_ap, ncols, nchunks_k, stg, stg_res, state):
    CH = stg[0].shape[1]
    for k in range(nchunks_k):
        c0 = 0
        if os.environ.get('K_NOWAW'):
            dst_res = Res()
        while c0 < ncols:
            w = min(CH, ncols - c0)
            i = state[0] % len(stg)
            state[0] += 1
            tr.dma("sp", "stg%d" % i, stg[i][:, 0:w], src_rows_ap[k * 128:(k + 1) * 128, c0:c0 + w], writes=[stg_res[i]])
            if True:
                tr.op("dve", lambda e, i=i, k=k, c0=c0, w=w: e.tensor_copy(out=dst[:, k, c0:c0 + w], in_=stg[i][:, 0:w]),
                      reads=[stg_res[i]], writes=[dst_res])
            else:
                tr.op("act", lambda e, i=i, k=k, c0=c0, w=w: e.activation(out=dst[:, k, c0:c0 + w], in_=stg[i][:, 0:w], func=AF.Copy),
                      reads=[stg_res[i]], writes=[dst_res])
            c0 += w


def phase1(nc, tr, l, S, x_src, w_in, w_uq, w_ukv, p_attn_g, p_gq, p_gkv, p_Gq, p_Gkn, p_Gkr, p_Ggq, p_Ggk,
           c_cm, c_sm, c_cg, c_sg, ident, ones_f, hT_d, qTm_d, kTm_d, vm_d, qTg_d, kTg_d, vg_d, pT_d):
    NT = S // 512
    with contextlib.ExitStack() as ps:
        uid = "p1_%d_" % next(_UID)
        sb = lambda n, s, d: ps.enter_context(nc.sbuf_tensor(uid + n, list(s), d))
        pm = lambda n, s, d: ps.enter_context(nc.psum_tensor(uid + n, list(s), d))
        if CUT < 0:
            return
        wi = sb("wi", [128, 8, 1696], BF16); r_wi = Res()
        wq = sb("wq", [128, 2, 768], BF16); r_wq = Res()
        wkv = sb("wkv", [128, 1, 1024], BF16); r_wkv = Res()
        with contextlib.ExitStack() as ps2:
            stg = [ps2.enter_context(nc.sbuf_tensor(uid + "stg%d" % i, [128, 1696], F32)) for i in range(2)]
            stg_res = [Res(), Res()]
            st = [0]
            load_cast(nc, tr, ps, "wi", wi, r_wi, w_in[l][:, 0:1696], 1696, int(os.environ.get("K_NCH", "8")), stg, stg_res, st)
            if os.environ.get("K_SKIP") != "wq":
                load_cast(nc, tr, ps, "wq", wq, r_wq, w_uq[l], 768, 2, stg, stg_res, st)
            if os.environ.get("K_SKIP") != "wkv":
                load_cast(nc, tr, ps, "wkv", wkv, r_wkv, w_ukv[l], 1024, 1, stg, stg_res, st)
            tr.barrier()
        if CUT < 1:
            return
        g_attn = sb("g_attn", [128, 8], F32); g_q = sb("g_q", [128, 2], F32); g_kv = sb("g_kv", [128, 1], F32)
        Gq = sb("Gq", [128, 8, 96], F32); Gkn = sb("Gkn", [128, 8, 64], F32); Gkr = sb("Gkr", [128, 8, 32], F32)
        Ggq = sb("Ggq", [128, 8, 64], F32); Ggk = sb("Ggk", [128, 2, 64], F32)
        r_par = Res()
        tr.dma("sp", "par", g_attn[:], p_attn_g[l], writes=[r_par])
        tr.dma("sp", "par", g_q[:], p_gq[l], writes=[r_par])
        tr.dma("sp", "par", g_kv[:], p_gkv[l], writes=[r_par])
        tr.dma("sp", "par", Gq[:].rearrange("p h d -> p (h d)"), p_Gq[l], writes=[r_par])
        tr.dma("sp", "par", Gkn[:].rearrange("p h d -> p (h d)"), p_Gkn[l], writes=[r_par])
        tr.dma("sp", "par", Gkr[:].rearrange("p h d -> p (h d)"), p_Gkr[l], writes=[r_par])
        tr.dma("sp", "par", Ggq[:].rearrange("p h d -> p (h d)"), p_Ggq[l], writes=[r_par])
        tr.dma("sp", "par", Ggk[:].rearrange("p h d -> p (h d)"), p_Ggk[l], writes=[r_par])

        xt = [sb("xt%d" % i, [128, 4, 1024], F32) for i in range(2)]; r_xt = [Res(), Res()]
        junk = sb("junk", [128, 1024], F32); r_junk = Res()
        ss = sb("ss", [128, 4], F32); r_ss = Res()
        rstd = sb("rstd", [128, 4], F32); r_rstd = Res()
        xn = sb("xn", [128, 4, 1024], BF16); r_xn = Res()
        hT = [sb("hT%d" % i, [128, 8, 512], BF16) for i in range(2)]; r_hT = [Res(), Res()]
        cqg = sb("cqg", [128, 2, 512], BF16); r_cqg = Res()
        ckvg = sb("ckvg", [128, 512], BF16); r_ckvg = Res()
        cqsq = sb("cqsq", [128, 2, 512], F32); r_cqsq = Res()
        ckvsq = sb("ckvsq", [128, 512], F32); r_ckvsq = Res()
        pst = [sb("pst%d" % i, [128, 512], F32) for i in range(2)]; r_pst = [Res(), Res()]
        cm = sb("cm", [128, 4, 128], F32); sm = sb("sm", [128, 4, 128], F32)
        cg = sb("cg", [128, 4, 256], F32); sg = sb("sg", [128, 4, 256], F32); r_tab = Res()
        rr = sb("rr", [128, 2], F32); r_rr = Res()
        hs = sb("hs", [128, 8], F32); r_hs = Res()
        hsq = sb("hsq", [128, 8], F32); r_hsq = Res()
        hsk = sb("hsk", [128, 2], F32); r_hsk = Res()
        sc = sb("sc", [128, 8], F32); r_sc = Res()
        rs1 = sb("rs1", [128, 1], F32); r_rs1 = Res()
        qn = sb("qn", [128, 8, 96], F32); r_qn = Res()
        t1 = sb("t1", [128, 8, 32], F32); t2 = sb("t2", [128, 8, 32], F32); r_t1 = Res(); r_t2 = Res()
        qb = sb("qb", [128, 8, 96], BF16); r_qb = Res()
        kb = sb("kb", [128, 8, 96], BF16); r_kb = Res()
        kr = sb("kr", [128, 8, 32], F32); r_kr = Res()
        gqn = sb("gqn", [128, 8, 64], F32); r_gqn = Res()
        gqb = sb("gqb", [128, 8, 64], BF16); r_gqb = Res()
        gkn = sb("gkn", [128, 2, 64], F32); r_gkn = Res()
        gkb = sb("gkb", [128, 2, 64], BF16); r_gkb = Res()
        qT_st = sb("qT_st", [128, 8, 512], BF16); r_qT_st = Res()
        kT_st = sb("kT_st", [128, 8, 512], BF16); r_kT_st = Res()
        gqT_st = sb("gqT_st", [128, 4, 512], BF16); r_gqT_st = Res()
        gkT_st = sb("gkT_st", [128, 512], BF16); r_gkT_st = Res()
        v_st = sb("v_st", [128, 4, 512], BF16); r_v_st = Res()
        gv_st = sb("gv_st", [128, 4, 128], BF16); r_gv_st = Res()
        tp = [pm("tp%d" % i, [128, 1024], BF16) for i in range(2)]; r_tp = [PRes(), PRes()]
        fm = [pm("fm%d" % i, [128, 512], F32) for i in range(2)]; r_fm = [PRes(), PRes()]
        A = pm("A", [128, 1024], F32); r_A = PRes(); r_ssq = r_A
        B = pm("B", [128, 1024], F32); r_B = PRes()
        tpi = [0]
        fmi = [0]

        def load_x(T):
            b = T % 2
            tr.dma("sp", "x%d" % b, xt[b][:], x_src[T * 512:(T + 1) * 512, :].rearrange("(s p) d -> p s d", p=128),
                   writes=[r_xt[b]])

        if CUT < 2:
            return
        load_x(0)
        for T in range(NT):
            b = T % 2
            if T + 1 < NT:
                load_x(T + 1)
            tsl = slice(T * 512, (T + 1) * 512)
            tr.dma("sp", "tab", cm[:], c_cm[tsl, :].rearrange("(s p) d -> p s d", p=128), writes=[r_tab])
            tr.dma("sp", "tab", sm[:], c_sm[tsl, :].rearrange("(s p) d -> p s d", p=128), writes=[r_tab])
            tr.dma("sp", "tab", cg[:], c_cg[tsl, :].rearrange("(s p) d -> p s d", p=128), writes=[r_tab])
            tr.dma("sp", "tab", sg[:], c_sg[tsl, :].rearrange("(s p) d -> p s d", p=128), writes=[r_tab])
            for s in range(4):
                tr.op("act", lambda e, s=s: e.activation(out=junk[:], in_=xt[b][:, s, :], func=AF.Square,
                                                          accum_out=ss[:, s:s + 1]),
                      reads=[r_xt[b]], writes=[r_junk, r_ss])
            tr.op("act", lambda e: e.activation(out=rstd[:], in_=ss[:], func=AF.Sqrt, scale=1.0 / D, bias=EPS),
                  reads=[r_ss], writes=[r_rstd])
            tr.op("dve", lambda e: e.reciprocal(out=rstd[:], in_=rstd[:]), reads=[r_rstd], writes=[r_rstd])
            for s in range(4):
                tr.op("dve", lambda e, s=s: e.tensor_scalar(out=xn[:, s, :], in0=xt[b][:, s, :], scalar1=rstd[:, s:s + 1],
                                                            scalar2=None, op0=ALU.mult),
                      reads=[r_xt[b], r_rstd], writes=[r_xn])
            for r in range(4):
                ti = tpi[0] % 2; tpi[0] += 1
                for cc in range(2):
                    for s in range(4):
                        c = 2 * r + cc
                        tr.op("pe", lambda e, c=c, s=s, cc=cc, ti=ti: e.transpose(
                            out=tp[ti][:, cc * 512 + s * 128: cc * 512 + (s + 1) * 128],
                            in_=xn[:, s, c * 128:(c + 1) * 128], identity=ident[:]),
                            reads=[r_xn, R_const_dummy], writes=[r_tp[ti]])
                for cc in range(2):
                    c = 2 * r + cc
                    eng = "act" if cc == 0 else "dve"
                    if eng == "act":
                        tr.op("act", lambda e, c=c, cc=cc, ti=ti: e.activation(out=hT[b][:, c, :], in_=tp[ti][:, cc * 512:(cc + 1) * 512],
                                                                                func=AF.Copy, scale=g_attn[:, c:c + 1]),
                              reads=[r_tp[ti], r_par], writes=[r_hT[b]])
                    else:
                        tr.op("dve", lambda e, c=c, cc=cc, ti=ti: e.tensor_scalar(out=hT[b][:, c, :], in0=tp[ti][:, cc * 512:(cc + 1) * 512],
                                                                                   scalar1=g_attn[:, c:c + 1], scalar2=None, op0=ALU.mult),
                              reads=[r_tp[ti], r_par], writes=[r_hT[b]])
            tr.dma("sp", "hTst%d" % b, hT_d[:, T * 512:(T + 1) * 512].rearrange("(c p) t -> p c t", p=128), hT[b][:],
                   reads=[r_hT[b]])
            if CUT < 3:
                continue
            fm_groups = [("cq", 0, 0), ("cq", 1, 128), ("ckv", 0, 256)] + [("pool", g, 1184 + g * 128) for g in range(4)]
            SUB = os.environ.get('K_SUB', '')
            for kind, idx, col in fm_groups:
                if kind in SUB.split(','):
                    continue
                fi = fmi[0] % 2; fmi[0] += 1
                for c in range(8):
                    tr.op("pe", lambda e, c=c, col=col, fi=fi: e.matmul(fm[fi][:], lhsT=wi[:, c, col:col + 128], rhs=hT[b][:, c, :],
                                                                       start=(c == 0), stop=(c == 7)),
                          reads=[r_wi, r_hT[b]], writes=[r_fm[fi]])
                if kind == "cq":
                    tr.op("act", lambda e, idx=idx, fi=fi: e.activation(out=cqsq[:, idx, :], in_=fm[fi][:], func=AF.Square),
                          reads=[r_fm[fi]], writes=[r_cqsq])
                    tr.op("dve", lambda e, idx=idx, fi=fi: e.tensor_scalar(out=cqg[:, idx, :], in0=fm[fi][:], scalar1=g_q[:, idx:idx + 1],
                                                                           scalar2=None, op0=ALU.mult),
                          reads=[r_fm[fi], r_par], writes=[r_cqg])
                elif kind == "ckv":
                    tr.op("act", lambda e, fi=fi: e.activation(out=ckvsq[:], in_=fm[fi][:], func=AF.Square),
                          reads=[r_fm[fi]], writes=[r_ckvsq])
                    tr.op("dve", lambda e, fi=fi: e.tensor_scalar(out=ckvg[:], in0=fm[fi][:], scalar1=g_kv[:, 0:1], scalar2=None, op0=ALU.mult),
                          reads=[r_fm[fi], r_par], writes=[r_ckvg])
                else:
                    pi = idx % 2
                    tr.op("act", lambda e, fi=fi, pi=pi: e.activation(out=pst[pi][:], in_=fm[fi][:], func=AF.Copy),
                          reads=[r_fm[fi]], writes=[r_pst[pi]])
                    tr.dma("sp", "pst%d" % pi, pT_d[idx * 128:(idx + 1) * 128, T * 512:(T + 1) * 512], pst[pi][:],
                           reads=[r_pst[pi]])
            if CUT < 4:
                continue
            for s in range(4):
                ssl = slice(s * 128, (s + 1) * 128)
                for (o0, c0, w) in ((0, 384, 512), (512, 896, 288)):
                    for c in range(8):
                        tr.op("pe", lambda e, c=c, o0=o0, c0=c0, w=w: e.matmul(A[:, o0:o0 + w], lhsT=hT[b][:, c, ssl], rhs=wi[:, c, c0:c0 + w],
                                                                              start=(c == 0), stop=(c == 7)),
                              reads=[r_wi, r_hT[b]], writes=[r_A])
                for m in range(2):
                    tr.op("pe", lambda e, m=m: e.matmul(A[:, 800:801], lhsT=cqsq[:, m, ssl], rhs=ones_f[:, 0:1], start=(m == 0), stop=(m == 1)),
                          reads=[r_cqsq], writes=[r_ssq])
                tr.op("pe", lambda e: e.matmul(A[:, 801:802], lhsT=ckvsq[:, ssl], rhs=ones_f[:, 0:1], start=True, stop=True),
                      reads=[r_ckvsq], writes=[r_ssq])
                for (o0, w) in ((0, 512), (512, 256)):
                    for m in range(2):
                        tr.op("pe", lambda e, m=m, o0=o0, w=w: e.matmul(B[:, o0:o0 + w], lhsT=cqg[:, m, ssl], rhs=wq[:, m, o0:o0 + w],
                                                                       start=(m == 0), stop=(m == 1)),
                              reads=[r_wq, r_cqg], writes=[r_B])
                tr.op("act", lambda e: e.activation(out=rr[:, 0:1], in_=A[:, 800:801], func=AF.Sqrt, scale=1.0 / 256, bias=EPS),
                      reads=[r_ssq], writes=[r_rr])
                tr.op("act", lambda e: e.activation(out=rr[:, 1:2], in_=A[:, 801:802], func=AF.Sqrt, scale=1.0 / 128, bias=EPS),
                      reads=[r_ssq], writes=[r_rr])
                tr.op("dve", lambda e: e.reciprocal(out=rr[:], in_=rr[:]), reads=[r_rr], writes=[r_rr])
                Agq = A[:, 32:544].rearrange("p (h d) -> p h d", h=8)
                Agk = A[:, 544:672].rearrange("p (h d) -> p h d", h=2)
                tr.op("act", lambda e: e.activation(out=gqn[:], in_=Agq, func=AF.Square), reads=[r_A], writes=[r_gqn])
                tr.op("act", lambda e: e.activation(out=gkn[:], in_=Agk, func=AF.Square), reads=[r_A], writes=[r_gkn])
                tr.op("dve", lambda e: e.reduce_sum(out=hsq[:], in_=gqn[:], axis=AX.X), reads=[r_gqn], writes=[r_hsq])
                tr.op("dve", lambda e: e.reduce_sum(out=hsk[:], in_=gkn[:], axis=AX.X), reads=[r_gkn], writes=[r_hsk])
                tr.op("act", lambda e: e.activation(out=hsq[:], in_=hsq[:], func=AF.Sqrt, scale=1.0 / 64, bias=EPS), reads=[r_hsq], writes=[r_hsq])
                tr.op("act", lambda e: e.activation(out=hsk[:], in_=hsk[:], func=AF.Sqrt, scale=1.0 / 64, bias=EPS), reads=[r_hsk], writes=[r_hsk])
                tr.op("dve", lambda e: e.reciprocal(out=hsq[:], in_=hsq[:]), reads=[r_hsq], writes=[r_hsq])
                tr.op("dve", lambda e: e.reciprocal(out=hsk[:], in_=hsk[:]), reads=[r_hsk], writes=[r_hsk])
                if CUT < 5:
                    continue
                Bq = B[:, 0:768].rearrange("p (h d) -> p h d", h=8)
                tr.op("act", lambda e: e.activation(out=qn[:], in_=Bq, func=AF.Square, scale=rr[:, 0:1]),
                      reads=[r_B, r_rr], writes=[r_qn])
                tr.op("dve", lambda e: e.reduce_sum(out=hs[:], in_=qn[:], axis=AX.X), reads=[r_qn], writes=[r_hs])
                tr.op("act", lambda e: e.activation(out=hs[:], in_=hs[:], func=AF.Sqrt, scale=1.0 / 96, bias=EPS), reads=[r_hs], writes=[r_hs])
                tr.op("dve", lambda e: e.reciprocal(out=hs[:], in_=hs[:]), reads=[r_hs], writes=[r_hs])
                tr.op("dve", lambda e: e.tensor_scalar(out=sc[:], in0=hs[:], scalar1=rr[:, 0:1], scalar2=None, op0=ALU.mult),
                      reads=[r_hs, r_rr], writes=[r_sc])
                for h in range(8):
                    tr.op("dve", lambda e, h=h: e.scalar_tensor_tensor(out=qn[:, h, :], in0=B[:, h * 96:(h + 1) * 96], scalar=sc[:, h:h + 1],
                                                                       in1=Gq[:, h, :], op0=ALU.mult, op1=ALU.mult),
                          reads=[r_B, r_sc, r_par], writes=[r_qn])
                for (o0, w) in ((0, 512), (512, 512)):
                    tr.op("pe", lambda e, o0=o0, w=w: e.matmul(B[:, o0:o0 + w], lhsT=ckvg[:, ssl], rhs=wkv[:, 0, o0:o0 + w], start=True, stop=True),
                          reads=[r_wkv, r_ckvg], writes=[r_B])
                cmv = cm[:, s, :].rearrange("p (h d) -> p h d", h=8)
                smv = sm[:, s, :].rearrange("p (h d) -> p h d", h=8)
                rope(tr, "dve", qn, r_qn, 64, 16, cmv, smv, r_tab, t1, r_t1, t2, r_t2, qb, r_qb)
                tr.op("dve", lambda e: e.tensor_copy(out=qb[:, :, 0:64], in_=qn[:, :, 0:64]), reads=[r_qn], writes=[r_qb])
                ti = tpi[0] % 2; tpi[0] += 1
                for h in range(8):
                    tr.op("pe", lambda e, h=h, ti=ti: e.transpose(out=tp[ti][0:96, h * 128:(h + 1) * 128], in_=qb[:, h, :], identity=ident[:]),
                          reads=[r_qb], writes=[r_tp[ti]])
                tr.op("act", lambda e, ti=ti: e.activation(out=qT_st[0:96, :, ssl], in_=tp[ti][0:96, :].rearrange("p (h t) -> p h t", h=8), func=AF.Copy),
                      reads=[r_tp[ti]], writes=[r_qT_st])
                if CUT < 6:
                    continue
                Bkv = B[:].rearrange("p (h d) -> p h d", h=8)
                tr.op("act", lambda e: e.activation(out=junk[:, 0:32], in_=A[:, 0:32], func=AF.Square, accum_out=rs1[:, 0:1]),
                      reads=[r_A], writes=[r_junk, r_rs1])
                tr.op("act", lambda e: e.activation(out=gqn[:], in_=Bkv[:, :, 0:64], func=AF.Square, scale=rr[:, 1:2]),
                      reads=[r_B, r_rr], writes=[r_gqn])
                tr.op("dve", lambda e: e.reduce_sum(out=hs[:], in_=gqn[:], axis=AX.X), reads=[r_gqn], writes=[r_hs])
                tr.op("dve", lambda e: e.tensor_scalar(out=hs[:], in0=hs[:], scalar1=rs1[:, 0:1], scalar2=None, op0=ALU.add),
                      reads=[r_hs, r_rs1], writes=[r_hs])
                tr.op("act", lambda e: e.activation(out=hs[:], in_=hs[:], func=AF.Sqrt, scale=1.0 / 96, bias=EPS), reads=[r_hs], writes=[r_hs])
                tr.op("dve", lambda e: e.reciprocal(out=hs[:], in_=hs[:]), reads=[r_hs], writes=[r_hs])
                tr.op("dve", lambda e: e.tensor_scalar(out=sc[:], in0=hs[:], scalar1=rr[:, 1:2], scalar2=None, op0=ALU.mult),
                      reads=[r_hs, r_rr], writes=[r_sc])
                for h in range(8):
                    tr.op("dve", lambda e, h=h: e.scalar_tensor_tensor(out=kb[:, h, 0:64], in0=B[:, h * 128:h * 128 + 64], scalar=sc[:, h:h + 1],
                                                                       in1=Gkn[:, h, :], op0=ALU.mult, op1=ALU.mult),
                          reads=[r_B, r_sc, r_par], writes=[r_kb])
                    tr.op("dve", lambda e, h=h: e.scalar_tensor_tensor(out=qn[:, h, 64:96], in0=A[:, 0:32], scalar=hs[:, h:h + 1],
                                                                       in1=Gkr[:, h, :], op0=ALU.mult, op1=ALU.mult),
                          reads=[r_A, r_hs, r_par], writes=[r_qn])
                rope(tr, "dve", qn, r_qn, 64, 16, cmv, smv, r_tab, t1, r_t1, t2, r_t2, kb, r_kb)
                tr.op("act", lambda e: e.activation(out=v_st[:, s, :].rearrange("p (h d) -> p h d", h=8), in_=Bkv[:, :, 64:128], func=AF.Copy,
                                                    scale=rr[:, 1:2]),
                      reads=[r_B, r_rr], writes=[r_v_st])
                ti = tpi[0] % 2; tpi[0] += 1
                for h in range(8):
                    tr.op("pe", lambda e, h=h, ti=ti: e.transpose(out=tp[ti][0:96, h * 128:(h + 1) * 128], in_=kb[:, h, :], identity=ident[:]),
                          reads=[r_kb], writes=[r_tp[ti]])
                tr.op("act", lambda e, ti=ti: e.activation(out=kT_st[0:96, :, ssl], in_=tp[ti][0:96, :].rearrange("p (h t) -> p h t", h=8), func=AF.Copy),
                      reads=[r_tp[ti]], writes=[r_kT_st])
                if CUT < 7:
                    continue
                for h in range(8):
                    tr.op("dve", lambda e, h=h: e.scalar_tensor_tensor(out=gqn[:, h, :], in0=A[:, 32 + h * 64: 32 + (h + 1) * 64], scalar=hsq[:, h:h + 1],
                                                                       in1=Ggq[:, h, :], op0=ALU.mult, op1=ALU.mult),
                          reads=[r_A, r_hsq, r_par], writes=[r_gqn])
                cgv = cg[:, s, :].rearrange("p (h d) -> p h d", h=8)
                sgv = sg[:, s, :].rearrange("p (h d) -> p h d", h=8)
                rope(tr, "dve", gqn, r_gqn, 0, 32, cgv, sgv, r_tab, t1, r_t1, t2, r_t2, gqb, r_gqb)
                ti = tpi[0] % 2; tpi[0] += 1
                for j in range(4):
                    tr.op("pe", lambda e, j=j, ti=ti: e.transpose(out=tp[ti][:, j * 128:(j + 1) * 128],
                                                                  in_=gqb[:, 2 * j:2 * j + 2, :].rearrange("p h d -> p (h d)"), identity=ident[:]),
                          reads=[r_gqb], writes=[r_tp[ti]])
                tr.op("act", lambda e, ti=ti: e.activation(out=gqT_st[:, :, ssl], in_=tp[ti][:, 0:512].rearrange("p (h t) -> p h t", h=4), func=AF.Copy),
                      reads=[r_tp[ti]], writes=[r_gqT_st])
                for h in range(2):
                    tr.op("dve", lambda e, h=h: e.scalar_tensor_tensor(out=gkn[:, h, :], in0=A[:, 544 + h * 64: 544 + (h + 1) * 64], scalar=hsk[:, h:h + 1],
                                                                       in1=Ggk[:, h, :], op0=ALU.mult, op1=ALU.mult),
                          reads=[r_A, r_hsk, r_par], writes=[r_gkn])
                rope(tr, "dve", gkn, r_gkn, 0, 32, cgv[:, 0:2, :], sgv[:, 0:2, :], r_tab, t1[:, 0:2, :], r_t1, t2[:, 0:2, :], r_t2, gkb, r_gkb)
                ti = tpi[0] % 2; tpi[0] += 1
                tr.op("pe", lambda e, ti=ti: e.transpose(out=tp[ti][:, 0:128], in_=gkb[:].rearrange("p h d -> p (h d)"), identity=ident[:]),
                      reads=[r_gkb], writes=[r_tp[ti]])
                tr.op("act", lambda e, ti=ti: e.activation(out=gkT_st[:, ssl], in_=tp[ti][:, 0:128], func=AF.Copy),
                      reads=[r_tp[ti]], writes=[r_gkT_st])
                tr.op("act", lambda e: e.activation(out=gv_st[:, s, :], in_=A[:, 672:800], func=AF.Copy), reads=[r_A], writes=[r_gv_st])
            if CUT < 8:
                continue
            tr.dma("sp", "st_q", qTm_d[:, :, tsl].rearrange("h d t -> d h t"), qT_st[0:96, :, :], reads=[r_qT_st])
            tr.dma("sp", "st_k", kTm_d[:, :, tsl].rearrange("h d t -> d h t"), kT_st[0:96, :, :], reads=[r_kT_st])
            tr.dma("sp", "st_gq", qTg_d[:, tsl].rearrange("(j p) t -> p j t", p=128), gqT_st[:], reads=[r_gqT_st])
            tr.dma("sp", "st_gk", kTg_d[:, tsl], gkT_st[:], reads=[r_gkT_st])
            tr.dma("sp", "st_v", vm_d[tsl, :].rearrange("(s p) f -> p s f", p=128), v_st[:], reads=[r_v_st])
            tr.dma("sp", "st_gv", vg_d[tsl, :].rearrange("(s p) f -> p s f", p=128), gv_st[:], reads=[r_gv_st])


R_const_dummy = Res()


def rope(tr, eng, src, r_src, off, half, cv, sv, r_tab, t1, r_t1, t2, r_t2, dst, r_dst):
    x1 = src[:, :, off:off + half]
    x2 = src[:, :, off + half:off + 2 * half]
    a = t1[:, :, 0:half]
    b2 = t2[:, :, 0:half]
    tr.op(eng, lambda e: e.tensor_tensor(out=a, in0=x1, in1=cv, op=ALU.mult), reads=[r_src, r_tab], writes=[r_t1])
    tr.op(eng, lambda e: e.tensor_tensor(out=b2, in0=x2, in1=sv, op=ALU.mult), reads=[r_src, r_tab], writes=[r_t2])
    tr.op(eng, lambda e: e.tensor_tensor(out=dst[:, :, off:off + half], in0=a, in1=b2, op=ALU.subtract), reads=[r_t1, r_t2], writes=[r_dst])
    tr.op(eng, lambda e: e.tensor_tensor(out=a, in0=x1, in1=sv, op=ALU.mult), reads=[r_src, r_tab], writes=[r_t1])
    tr.op(eng, lambda e: e.tensor_tensor(out=b2, in0=x2, in1=cv, op=ALU.mult), reads=[r_src, r_tab], writes=[r_t2])
    tr.op(eng, lambda e: e.tensor_tensor(out=dst[:, :, off + half:off + 2 * half], in0=a, in1=b2, op=ALU.add), reads=[r_t1, r_t2], writes=[r_dst])


def phase2(nc, tr, S, ident, ones_f, qTm_d, kTm_d, vm_d, qTg_d, kTg_d, vg_d, oT_d, rec_d):
    NK = S // 128
    NQ = S // 1024
    with contextlib.ExitStack() as ps:
        uid = "p2_%d_" % next(_UID)
        sb = lambda n, s, d: ps.enter_context(nc.sbuf_tensor(uid + n, list(s), d))
        pm = lambda n, s, d: ps.enter_context(nc.psum_tensor(uid + n, list(s), d))
        qt = [sb("qt%d" % i, [128, S], BF16) for i in range(2)]
        kt = [sb("kt%d" % i, [128, S], BF16) for i in range(2)]
        vt = [sb("vt%d" % i, [128, NK, 65], BF16) for i in range(2)]
        r_q = [Res(), Res()]; r_k = [Res(), Res()]; r_v = [Res(), Res()]
        pb = [sb("pb%d" % i, [128, 1024], BF16) for i in range(2)]; r_pb = [Res(), Res()]
        osb = sb("osb", [128, 1024], F32); r_osb = Res()
        rec = sb("rec", [128, 1024], F32); r_rec = Res()
        recb = sb("recb", [64, 1024], F32); r_recb = Res()
        r_recd = [Res(), Res()]
        on = [sb("on%d" % i, [64, 1024], BF16) for i in range(2)]; r_on = [Res(), Res()]
        Sp = [pm("S%d" % i, [128, 1024], F32) for i in range(2)]; r_S = [PRes(), PRes()]
        Ops = [pm("O%d" % i, [128, 1024], F32) for i in range(2)]; r_Os = [PRes(), PRes()]
        for i in range(2):
            tr.op("dve", lambda e, i=i: e.memset(vt[i][:, :, 64:65], 1.0), writes=[r_v[i]])
            tr.op("dve", lambda e, i=i: e.memset(qt[i][64:128, :], 0.0), writes=[r_q[i]])
            tr.op("dve", lambda e, i=i: e.memset(kt[i][64:128, :], 0.0), writes=[r_k[i]])

        def heads():
            for h in range(8):
                yield ("m", h, 96, qTm_d[h], kTm_d[h], vm_d[:, h * 64:(h + 1) * 64], h * 64)
            for j in range(8):
                g = j // 4
                yield ("g", j, 64, qTg_d[j * 64:(j + 1) * 64, :], kTg_d[g * 64:(g + 1) * 64, :], vg_d[:, g * 64:(g + 1) * 64], 512 + j * 64)

        hl = list(heads())

        def load(i):
            kind, h, dk, qa, ka, va, orow = hl[i]
            sl = i % 2
            if kind == "g" and h < 2:
                tr.op("dve", lambda e: e.memset(qt[sl][64:128, :], 0.0), writes=[r_q[sl]])
                tr.op("dve", lambda e: e.memset(kt[sl][64:128, :], 0.0), writes=[r_k[sl]])
            tr.dma("sp", "q%d" % sl, qt[sl][0:dk, :], qa, writes=[r_q[sl]])
            tr.dma("sp", "k%d" % sl, kt[sl][0:dk, :], ka, writes=[r_k[sl]])
            tr.dma("sp", "v%d" % sl, vt[sl][:, :, 0:64], va.rearrange("(k p) d -> p k d", p=128), writes=[r_v[sl]])

        load(0)
        it = 0
        oni = 0
        for i, (kind, h, dk, qa, ka, va, orow) in enumerate(hl):
            sl = i % 2
            if i + 1 < len(hl):
                load(i + 1)
            scale = float(dk) ** -0.5
            for qp in range(NQ):
                q0 = qp * 1024
                Op = Ops[oni % 2]; r_O = r_Os[oni % 2]

                def qk(ki, it):
                    sbi = it % 2
                    for half in range(2):
                        tr.op("pe", lambda e, half=half: e.matmul(Sp[sbi][:, half * 512:(half + 1) * 512],
                                                                 lhsT=kt[sl][:, ki * 128:(ki + 1) * 128],
                                                                 rhs=qt[sl][:, q0 + half * 512: q0 + (half + 1) * 512],
                                                                 start=True, stop=True),
                              reads=[r_q[sl], r_k[sl]], writes=[r_S[sbi]], embed=True)

                def ex(ki, it):
                    sbi = it % 2
                    tr.op("act", lambda e: e.activation(out=pb[sbi][:], in_=Sp[sbi][:], func=AF.Exp, scale=scale),
                          reads=[r_S[sbi]], writes=[r_pb[sbi]], skip_own=True, embed=True)

                def pv(ki, it):
                    sbi = it % 2
                    for half in range(2):
                        tr.op("pe", lambda e, half=half: e.matmul(Op[0:65, half * 512:(half + 1) * 512],
                                                                 lhsT=vt[sl][:, ki, 0:65],
                                                                 rhs=pb[sbi][:, half * 512:(half + 1) * 512],
                                                                 start=(ki == 0), stop=(ki == NK - 1)),
                              reads=[r_v[sl], r_pb[sbi]], writes=[r_O], embed=True)

                qk(0, it)
                for ki in range(NK):
                    if ki + 1 < NK:
                        qk(ki + 1, it + ki + 1)
                    ex(ki, it + ki)
                    pv(ki, it + ki)
                it += NK
                tr.op("dve", lambda e: e.tensor_copy(out=osb[0:65, :], in_=Op[0:65, :]), reads=[r_O], writes=[r_osb])
                tr.op("dve", lambda e: e.reciprocal(out=rec[64:65, :], in_=osb[64:65, :]), reads=[r_osb], writes=[r_rec])
                oi = oni % 2; oni += 1
                tr.dma("sp", "recw%d" % oi, rec_d[oi:oi + 1, :], rec[64:65, :], reads=[r_rec], writes=[r_recd[oi]])
                tr.dma("sp", "recr", recb[:], bass.AP(rec_d.tensor, oi * 1024, [[0, 64], [1, 1024]]), reads=[r_recd[oi]], writes=[r_recb])
                tr.op("dve", lambda e, oi=oi: e.tensor_tensor(out=on[oi][:], in0=osb[0:64, :], in1=recb[:], op=ALU.mult),
                      reads=[r_osb, r_recb], writes=[r_on[oi]])
                tr.dma("sp", "on%d" % oi, oT_d[orow:orow + 64, q0:q0 + 1024], on[oi][:], reads=[r_on[oi]])


def phase3(nc, tr, l, S, x_src, out, w_in, pool_w, w_bm, w_bg, w_bp, w_out, p_pscale, p_ffn_g, c_invc, ident, hT_d, pT_d, oT_d):
    TT = 512
    NT = S // TT
    NS = TT // 128
    W = TT + 16
    with contextlib.ExitStack() as ps:
        uid = "p3_%d_" % next(_UID)
        sb = lambda n, s, d: ps.enter_context(nc.sbuf_tensor(uid + n, list(s), d))
        pm = lambda n, s, d: ps.enter_context(nc.psum_tensor(uid + n, list(s), d))
        wg = sb("wg", [128, 8, 3072], BF16); r_wg = Res()
        wb = [sb("wb%d" % i, [128, 4, 1024], BF16) for i in range(3)]; r_wb = [Res(), Res(), Res()]
        wo = sb("wo", [128, 8, 1024], BF16); r_wo = Res()
        wp = sb("wp", [128, 4, 128], BF16); r_wp = Res()
        with contextlib.ExitStack() as ps2:
            stg = [ps2.enter_context(nc.sbuf_tensor(uid + "stg%d" % i, [128, 2048], F32)) for i in range(2)]
            stg_res = [Res(), Res()]
            st = [0]
            load_cast(nc, tr, ps, "wg", wg, r_wg, w_in[l][:, 1696:4768], 3072, 8, stg, stg_res, st)
            for i, wsrc in enumerate((w_bm, w_bg, w_bp)):
                load_cast(nc, tr, ps, "wb", wb[i], r_wb[i], wsrc[l], 1024, 4, stg, stg_res, st)
            load_cast(nc, tr, ps, "wo", wo, r_wo, w_out[l], 1024, 8, stg, stg_res, st)
            load_cast(nc, tr, ps, "wp", wp, r_wp, pool_w[l].rearrange("g c d -> (g c) d"), 128, 4, stg, stg_res, st)
            tr.barrier()
        pscale = sb("pscale", [128, 4], F32); g_ffn = sb("g_ffn", [128, 8], F32); r_par = Res()
        tr.dma("sp", "par", pscale[:], p_pscale[l], writes=[r_par])
        tr.dma("sp", "par", g_ffn[:], p_ffn_g[l], writes=[r_par])

        hT = [sb("hT%d" % i, [128, 8, TT], BF16) for i in range(2)]; r_hT = [Res(), Res()]
        oT = [sb("oT%d" % i, [128, 8, TT], BF16) for i in range(2)]; r_oT = [Res(), Res()]
        pin = sb("pin", [128, 4, W], F32); r_pin = Res()
        ivc = sb("ivc", [128, 4, TT], F32); r_ivc = Res()
        xs = [sb("xs%d" % i, [128, 1024], F32) for i in range(NS)]; r_xs = [Res() for _ in range(NS)]
        tA = sb("tA", [128, W], F32); r_tA = Res()
        tB = sb("tB", [128, W], F32); r_tB = Res()
        mxg = sb("mxg", [128, TT], F32); r_mxg = Res()
        mxb = [sb("mxb%d" % i, [128, 4, TT], BF16) for i in range(2)]; r_mxb = [Res(), Res()]
        opl = sb("opl", [128, 4, TT], BF16); r_opl = Res()
        sg = [sb("sg%d" % i, [128, TT], F32) for i in range(3)]; r_sg = [Res(), Res(), Res()]
        mm = sb("mm", [128, TT], F32); r_mm = Res()
        mg = sb("mg", [128, 8, TT], BF16); r_mg = Res()
        junk = sb("junk", [128, 1024], F32); r_junk = Res()
        ss = sb("ss", [128, NS], F32); r_ss = Res()
        rstd = sb("rstd", [128, NS], F32); r_rstd = Res()
        xn = sb("xn", [128, NS, 1024], BF16); r_xn = Res()
        Y = [pm("Y%d" % i, [128, 512], F32) for i in range(3)]; r_Y = [PRes(), PRes(), PRes()]
        Gp = [pm("G%d" % i, [128, 512], F32) for i in range(3)]; r_G = [PRes(), PRes(), PRes()]
        M0 = pm("M0", [128, 512], F32); r_M0 = PRes()
        M1 = pm("M1", [128, 512], F32); r_M1 = PRes()
        tpb = M1[:].bitcast(BF16)

        def load_a(T):
            b = T % 2
            t0 = T * TT
            tr.dma("sp", "h%d" % b, hT[b][:], hT_d[:, t0:t0 + TT].rearrange("(c p) t -> p c t", p=128), writes=[r_hT[b]])
            tr.dma("sp", "o%d" % b, oT[b][:], oT_d[:, t0:t0 + TT].rearrange("(c p) t -> p c t", p=128), writes=[r_oT[b]])

        def load_b(T):
            t0 = T * TT
            lo = max(t0 - 8, 0); hi = min(t0 + TT + 8, S)
            if lo != t0 - 8:
                tr.op("dve", lambda e: e.memset(pin[:, :, 0:8], 0.0), writes=[r_pin])
            if hi != t0 + TT + 8:
                tr.op("dve", lambda e: e.memset(pin[:, :, TT + 8:TT + 16], 0.0), writes=[r_pin])
            tr.dma("sp", "p0", pin[:, :, lo - (t0 - 8): hi - (t0 - 8)], pT_d[:, lo:hi].rearrange("(g p) t -> p g t", p=128), writes=[r_pin])
            tr.dma("sp", "i0", ivc[:], bass.AP(c_invc.tensor, t0, [[0, 128], [S, 4], [1, TT]]), writes=[r_ivc])

        def pool_chain(T, g):
            P = pin[:, g, :]
            tr.op("dve", lambda e, P=P: e.tensor_tensor(out=tA[:, 0:W - 1], in0=P[:, 0:W - 1], in1=P[:, 1:W], op=ALU.add),
                  reads=[r_pin], writes=[r_tA])
            src, rsrc, j0 = tA, r_tA, 7
            if g >= 1:
                tr.op("dve", lambda e: e.tensor_tensor(out=tB[:, 0:W - 3], in0=tA[:, 0:W - 3], in1=tA[:, 2:W - 1], op=ALU.add),
                      reads=[r_tA], writes=[r_tB])
                src, rsrc, j0 = tB, r_tB, 6
            if g >= 2:
                tr.op("dve", lambda e: e.tensor_tensor(out=tA[:, 0:W - 7], in0=tB[:, 0:W - 7], in1=tB[:, 4:W - 3], op=ALU.add),
                      reads=[r_tB], writes=[r_tA])
                src, rsrc, j0 = tA, r_tA, 4
            if g >= 3:
                tr.op("dve", lambda e: e.tensor_tensor(out=tB[:, 0:W - 15], in0=tA[:, 0:W - 15], in1=tA[:, 8:W - 7], op=ALU.add),
                      reads=[r_tA], writes=[r_tB])
                src, rsrc, j0 = tB, r_tB, 0
            tr.op("dve", lambda e, g=g, src=src, j0=j0: e.tensor_tensor(out=mxg[:], in0=src[:, j0:j0 + TT], in1=ivc[:, g, :], op=ALU.mult),
                  reads=[rsrc, r_ivc], writes=[r_mxg])
            tr.op("dve", lambda e, g=g, P=P: e.tensor_tensor(out=mxb[T % 2][:, g, :], in0=mxg[:], in1=P[:, 8:8 + TT], op=ALU.subtract),
                  reads=[r_mxg, r_pin], writes=[r_mxb[T % 2]])

        load_a(0)
        load_b(0)
        for g in range(4):
            pool_chain(0, g)
        if NT > 1:
            load_b(1)
        for T in range(NT):
            b = T % 2
            t0 = T * TT
            if T + 1 < NT:
                load_a(T + 1)
            for s in range(NS):
                tr.dma("sp", "x%d" % s, xs[s][:], x_src[t0 + s * 128: t0 + (s + 1) * 128, :], writes=[r_xs[s]])
            for g in range(4):
                tr.op("pe", lambda e, g=g: e.matmul(M0[:, 0:TT], lhsT=wp[:, g, :], rhs=mxb[b][:, g, :], start=True, stop=True),
                      reads=[r_wp, r_mxb[b]], writes=[r_M0])
                tr.op("act", lambda e, g=g: e.activation(out=opl[:, g, :], in_=M0[:, 0:TT], func=AF.Copy, scale=pscale[:, g:g + 1]),
                      reads=[r_M0, r_par], writes=[r_opl])
            srcs = ((oT[b], r_oT[b], 0), (oT[b], r_oT[b], 4), (opl, r_opl, 0))
            for n in range(8):
                nsl = slice(n * 128, (n + 1) * 128)
                for br in range(3):
                    for c in range(8):
                        tr.op("pe", lambda e, br=br, c=c: e.matmul(Gp[br][:, 0:TT], lhsT=wg[:, c, br * 1024 + n * 128: br * 1024 + (n + 1) * 128],
                                                                  rhs=hT[b][:, c, :], start=(c == 0), stop=(c == 7)),
                              reads=[r_wg, r_hT[b]], writes=[r_G[br]])
                    tr.op("act", lambda e, br=br: e.activation(out=sg[br][:], in_=Gp[br][:, 0:TT], func=AF.Sigmoid),
                          reads=[r_G[br]], writes=[r_sg[br]])
                for br, (src, rsrc, coff) in enumerate(srcs):
                    for c in range(4):
                        tr.op("pe", lambda e, br=br, c=c, src=src, coff=coff: e.matmul(Y[br][:, 0:TT], lhsT=wb[br][:, c, nsl], rhs=src[:, coff + c, :],
                                                                                      start=(c == 0), stop=(c == 3)),
                              reads=[r_wb[br], rsrc], writes=[r_Y[br]])
                tr.op("dve", lambda e: e.tensor_tensor(out=mm[:], in0=sg[0][:], in1=Y[0][:, 0:TT], op=ALU.mult),
                      reads=[r_sg[0], r_Y[0]], writes=[r_mm])
                tr.op("dve", lambda e: e.tensor_tensor(out=sg[1][:], in0=sg[1][:], in1=Y[1][:, 0:TT], op=ALU.mult),
                      reads=[r_sg[1], r_Y[1]], writes=[r_sg[1]])
                tr.op("dve", lambda e: e.tensor_tensor(out=sg[2][:], in0=sg[2][:], in1=Y[2][:, 0:TT], op=ALU.mult),
                      reads=[r_sg[2], r_Y[2]], writes=[r_sg[2]])
                tr.op("dve", lambda e: e.tensor_tensor(out=mm[:], in0=mm[:], in1=sg[1][:], op=ALU.add), reads=[r_mm, r_sg[1]], writes=[r_mm])
                tr.op("dve", lambda e, n=n: e.tensor_tensor(out=mg[:, n, :], in0=mm[:], in1=sg[2][:], op=ALU.add), reads=[r_mm, r_sg[2]], writes=[r_mg])
                if n < 4 and T + 1 < NT:
                    pool_chain(T + 1, n)
                    if n == 3 and T + 2 < NT:
                        load_b(T + 2)
            for s in range(NS):
                for half in range(2):
                    Mx, rMx = (M0, r_M0) if half == 0 else (M1, r_M1)
                    for c in range(8):
                        tr.op("pe", lambda e, c=c, Mx=Mx, half=half, s=s: e.matmul(Mx[:], lhsT=mg[:, c, s * 128:(s + 1) * 128], rhs=wo[:, c, half * 512:(half + 1) * 512],
                                                                                  start=(c == 0), stop=(c == 7)),
                              reads=[r_wo, r_mg], writes=[rMx])
                    tr.op("dve", lambda e, Mx=Mx, half=half, s=s: e.tensor_tensor(out=xs[s][:, half * 512:(half + 1) * 512], in0=xs[s][:, half * 512:(half + 1) * 512],
                                                                                  in1=Mx[:], op=ALU.add),
                          reads=[rMx, r_xs[s]], writes=[r_xs[s]])
                tr.dma("sp", "xo%d" % s, out[t0 + s * 128: t0 + (s + 1) * 128, :], xs[s][:], reads=[r_xs[s]])
                tr.op("act", lambda e, s=s: e.activation(out=junk[:], in_=xs[s][:], func=AF.Square, accum_out=ss[:, s:s + 1]),
                      reads=[r_xs[s]], writes=[r_junk, r_ss])
            tr.op("act", lambda e: e.activation(out=rstd[:], in_=ss[:], func=AF.Sqrt, scale=1.0 / D, bias=EPS), reads=[r_ss], writes=[r_rstd])
            tr.op("dve", lambda e: e.reciprocal(out=rstd[:], in_=rstd[:]), reads=[r_rstd], writes=[r_rstd])
            for s in range(NS):
                tr.op("dve", lambda e, s=s: e.tensor_scalar(out=xn[:, s, :], in0=xs[s][:], scalar1=rstd[:, s:s + 1], scalar2=None, op0=ALU.mult),
                      reads=[r_xs[s], r_rstd], writes=[r_xn])
            for r in range(4):
                for cc in range(2):
                    c = 2 * r + cc
                    for s in range(NS):
                        tr.op("pe", lambda e, c=c, cc=cc, s=s: e.transpose(out=tpb[:, cc * TT + s * 128: cc * TT + (s + 1) * 128],
                                                                           in_=xn[:, s, c * 128:(c + 1) * 128], identity=ident[:]),
                              reads=[r_xn], writes=[r_M1])
                for cc in range(2):
                    c = 2 * r + cc
                    if cc % 2 == 0:
                        tr.op("act", lambda e, c=c, cc=cc: e.activation(out=hT[b][:, c, :], in_=tpb[:, cc * TT:(cc + 1) * TT], func=AF.Copy, scale=g_ffn[:, c:c + 1]),
                              reads=[r_M1, r_par], writes=[r_hT[b]])
                    else:
                        tr.op("dve", lambda e, c=c, cc=cc: e.tensor_scalar(out=hT[b][:, c, :], in0=tpb[:, cc * TT:(cc + 1) * TT], scalar1=g_ffn[:, c:c + 1],
                                                                           scalar2=None, op0=ALU.mult),
                              reads=[r_M1, r_par], writes=[r_hT[b]])
            tr.dma("sp", "hf%d" % b, hT_d[:, t0:t0 + TT].rearrange("(c p) t -> p c t", p=128), hT[b][:], reads=[r_hT[b]])


def phase4(nc, tr, l, S, out, w_up, w_down, p_convw, p_convb, hT_d):
    TT = 512
    NT = S // TT
    NS = TT // 128
    with contextlib.ExitStack() as ps:
        uid = "p4_%d_" % next(_UID)
        sb = lambda n, s, d: ps.enter_context(nc.sbuf_tensor(uid + n, list(s), d))
        pm = lambda n, s, d: ps.enter_context(nc.psum_tensor(uid + n, list(s), d))
        wu = sb("wu", [128, 8, 2 * DFF], BF16); r_wu = Res()
        wd = sb("wd", [128, 22, 1024], BF16); r_wd = Res()
        with contextlib.ExitStack() as ps2:
            stg = [ps2.enter_context(nc.sbuf_tensor(uid + "stg%d" % i, [128, 2816], F32)) for i in range(2)]
            stg_res = [Res(), Res()]
            st = [0]
            load_cast(nc, tr, ps2, "wu", wu, r_wu, w_up[l], 2 * DFF, 8, stg, stg_res, st)
            load_cast(nc, tr, ps2, "wd", wd, r_wd, w_down[l], 1024, 22, stg, stg_res, st)
            tr.barrier()
        cw = sb("cw", [128, 3, 44], F32); cb = sb("cb", [128, 44], F32); r_par = Res()
        tr.dma("sp", "par", cw[:], p_convw[l], writes=[r_par])
        tr.dma("sp", "par", cb[:], p_convb[l], writes=[r_par])
        hf = [sb("hf%d" % i, [128, 8, TT + 2], BF16) for i in range(2)]; r_hf = [Res(), Res()]
        xs = [sb("xs%d" % i, [128, 1024], F32) for i in range(NS)]; r_xs = [Res() for _ in range(NS)]
        aT = sb("aT", [128, 22, TT], BF16); r_aT = Res()
        y0 = [sb("y0_%d" % i, [128, TT], F32) for i in range(2)]; r_y0 = [Res(), Res()]
        y1 = [sb("y1_%d" % i, [128, TT], F32) for i in range(2)]; r_y1 = [Res(), Res()]
        U = [pm("U%d" % i, [128, 1024], F32) for i in range(3)]; r_U = [PRes() for _ in range(3)]
        Dn = [pm("Dn%d" % i, [128, 512], F32) for i in range(2)]; r_Dn = [PRes() for _ in range(2)]

        def load(T):
            b = T % 2
            t0 = T * TT
            lo = max(t0 - 1, 0); hi = min(t0 + TT + 1, S)
            if lo != t0 - 1:
                tr.op("dve", lambda e: e.memset(hf[b][:, :, 0:1], 0.0), writes=[r_hf[b]])
            if hi != t0 + TT + 1:
                tr.op("dve", lambda e: e.memset(hf[b][:, :, TT + 1:TT + 2], 0.0), writes=[r_hf[b]])
            tr.dma("sp", "h%d" % b, hf[b][:, :, lo - (t0 - 1): hi - (t0 - 1)], hT_d[:, lo:hi].rearrange("(c p) t -> p c t", p=128), writes=[r_hf[b]])

        load(0)
        ui = 0
        di = 0
        for T in range(NT):
            b = T % 2
            t0 = T * TT
            if T + 1 < NT:
                load(T + 1)
            for s in range(NS):
                tr.dma("sp", "x%d" % s, xs[s][:], out[t0 + s * 128: t0 + (s + 1) * 128, :], writes=[r_xs[s]])
            for pr in range(22):
                for which, cf in ((0, pr), (1, pr + 22)):
                    u = ui % 3; ui += 1
                    for (c0, c1) in ((0, 512), (512, TT + 2)):
                        for c in range(8):
                            tr.op("pe", lambda e, c=c, u=u, cf=cf, c0=c0, c1=c1: e.matmul(U[u][:, c0:c1], lhsT=wu[:, c, cf * 128:(cf + 1) * 128], rhs=hf[b][:, c, c0:c1],
                                                                                       start=(c == 0), stop=(c == 7)),
                                  reads=[r_wu, r_hf[b]], writes=[r_U[u]])
                    tr.op("act", lambda e, u=u, cf=cf, which=which: e.activation(out=y0[which][:], in_=U[u][:, 1:TT + 1], func=AF.Identity,
                                                                                 scale=cw[:, 1, cf:cf + 1], bias=cb[:, cf:cf + 1]),
                          reads=[r_U[u], r_par], writes=[r_y0[which]])
                    tr.op("dve", lambda e, u=u, cf=cf, which=which: e.scalar_tensor_tensor(out=y1[which][:], in0=U[u][:, 0:TT], scalar=cw[:, 0, cf:cf + 1],
                                                                                           in1=y0[which][:], op0=ALU.mult, op1=ALU.add),
                          reads=[r_U[u], r_par, r_y0[which]], writes=[r_y1[which]])
                    tr.op("dve", lambda e, u=u, cf=cf, which=which: e.scalar_tensor_tensor(out=y0[which][:], in0=U[u][:, 2:TT + 2], scalar=cw[:, 2, cf:cf + 1],
                                                                                           in1=y1[which][:], op0=ALU.mult, op1=ALU.add),
                          reads=[r_U[u], r_par, r_y1[which]], writes=[r_y0[which]])
                tr.op("act", lambda e: e.activation(out=y1[0][:], in_=y0[0][:], func=AF.Silu), reads=[r_y0[0]], writes=[r_y1[0]])
                tr.op("dve", lambda e, pr=pr: e.tensor_tensor(out=aT[:, pr, :], in0=y1[0][:], in1=y0[1][:], op=ALU.mult),
                      reads=[r_y1[0], r_y0[1]], writes=[r_aT])
            for s in range(NS):
                for half in range(2):
                    d = di % 2; di += 1
                    for c in range(22):
                        tr.op("pe", lambda e, c=c, d=d, half=half: e.matmul(Dn[d][:], lhsT=aT[:, c, s * 128:(s + 1) * 128], rhs=wd[:, c, half * 512:(half + 1) * 512],
                                                                           start=(c == 0), stop=(c == 21)),
                              reads=[r_wd, r_aT], writes=[r_Dn[d]])
                    tr.op("dve", lambda e, d=d, half=half, s=s: e.tensor_tensor(out=xs[s][:, half * 512:(half + 1) * 512], in0=xs[s][:, half * 512:(half + 1) * 512],
                                                                                in1=Dn[d][:], op=ALU.add),
                          reads=[r_Dn[d], r_xs[s]], writes=[r_xs[s]])
                tr.dma("sp", "xo%d" % s, out[t0 + s * 128: t0 + (s + 1) * 128, :], xs[s][:], reads=[r_xs[s]])


_NC_CACHE = {}


def make_in_maps(inputs, S, L, ncores):
    consts = host_consts(S)
    lay = host_layout(inputs, L)
    f = lambda a: np.ascontiguousarray(a, dtype=np.float32)
    shared = {
        "w_in": f(inputs["w_in"]), "mla_w_uq": f(inputs["mla_w_uq"]), "mla_w_ukv": f(inputs["mla_w_ukv"]),
        "pool_w": f(inputs["pool_w"]), "w_branch_mla": f(inputs["w_branch_mla"]), "w_branch_gqa": f(inputs["w_branch_gqa"]),
        "w_branch_pool": f(inputs["w_branch_pool"]), "w_out": f(inputs["w_out"]), "ffn_w_up": f(inputs["ffn_w_up"]),
        "ffn_w_down": f(inputs["ffn_w_down"]),
    }
    shared.update(lay)
    shared.update(consts)
    x = np.asarray(inputs["x"], dtype=np.float32)
    maps = []
    for c in range(ncores):
        m = dict(shared)
        m["x"] = np.ascontiguousarray(x[c])
        maps.append(m)
    return maps


def kernel(**inputs):
    x = np.asarray(inputs["x"])
    B, S, _ = x.shape
    L = int(np.asarray(inputs["w_in"]).shape[0])
    key = (S, L)
    if key not in _NC_CACHE:
        _NC_CACHE[key] = build_nc(S, L)
    nc = _NC_CACHE[key]
    maps = make_in_maps(inputs, S, L, B)
    res = run_bass_kernel_spmd(nc, maps, core_ids=list(range(B)))
    return np.stack([np.asarray(res.results[c]["out"], dtype=np.float32) for c in range(B)], axis=0)
```
